# Optimizing a Trainium2 kernel written in Bass

```python
import math
import jax, jax.numpy as jnp
from jax import lax
import numpy as np

D_MODEL = 4096
BATCH = 8
SEQ = 2048
DEPTH = 2

D_MIX = D_MODEL
N_GROUPS = 4
GROUP_W = D_MIX // N_GROUPS
HEAD_DIM = 128

GLA_HEADS = 4
GLA_DV = GROUP_W // GLA_HEADS
GLA_DK = GLA_DV // 2
GLA_KEY = GLA_HEADS * GLA_DK
GLA_GATE_RANK = 16
GLA_GATE_NORMALIZER = 16.0
GLA_CHUNK = 64

SWA_HEADS = GROUP_W // HEAD_DIM
SWA_KV_HEADS = 2
SWA_KV = SWA_KV_HEADS * HEAD_DIM
SWA_WINDOW = 128
SWA_BLOCK = 128

MOBA_HEADS = GROUP_W // HEAD_DIM
MOBA_BLOCK = 256
MOBA_TOPK = 3
MOBA_Q_CHUNK = 16

SB_HEADS = GROUP_W // HEAD_DIM
SB_Q_BLOCK = 128

ROPE_THETA = 10000.0
D_FF = 11008
FFN_RES = 0.5
LN_EPS = 1e-5
RMS_EPS = 1e-5
DN_ALPHA = (2 * DEPTH) ** 0.25
DN_BETA = (8 * DEPTH) ** -0.25

IN_SPLIT_SIZES = (
    GLA_KEY, GLA_KEY, GROUP_W, GROUP_W, GLA_GATE_RANK,
    GROUP_W, SWA_KV, SWA_KV,
    GROUP_W, GROUP_W, GROUP_W,
    GROUP_W, GROUP_W, GROUP_W,
)
D_IN = sum(IN_SPLIT_SIZES)

kernel_name = "hybrid_parallel_heads_gla_swa_moba_stickbreak_macaron_deepnorm"


def layer_norm(x, g, b):
    xf = x.astype(jnp.float32)
    mu = jnp.mean(xf, axis=-1, keepdims=True)
    var = jnp.mean(jnp.square(xf - mu), axis=-1, keepdims=True)
    return ((xf - mu) * lax.rsqrt(var + LN_EPS) * g.astype(jnp.float32) + b.astype(jnp.float32)).astype(x.dtype)


def swiglu(h, w_gu, w_down):
    gate, up = jnp.split(h @ w_gu, 2, axis=-1)
    return (jax.nn.silu(gate) * up) @ w_down


def rope_tables(positions):
    inv = 1.0 / (ROPE_THETA ** (jnp.arange(0, HEAD_DIM, 2, dtype=jnp.float32) / HEAD_DIM))
    ang = positions.astype(jnp.float32)[..., None] * inv
    return jnp.cos(ang)[:, :, None, :], jnp.sin(ang)[:, :, None, :]


def apply_rope(x, cos, sin):
    xf = x.astype(jnp.float32)
    x1, x2 = jnp.split(xf, 2, axis=-1)
    return jnp.concatenate([x1 * cos - x2 * sin, x2 * cos + x1 * sin], axis=-1).astype(x.dtype)


def gla_mixer(q, k, v, g, gate_lr, w_gate_up, b_gate_up, norm_g):
    B, S, _ = q.shape
    f32 = jnp.float32
    C = GLA_CHUNK

    def chunks(t, d):
        return t.astype(f32).reshape(B, S // C, C, GLA_HEADS, d).transpose(1, 0, 3, 2, 4)

    log_decay = jax.nn.log_sigmoid((gate_lr @ w_gate_up + b_gate_up).astype(f32)) / GLA_GATE_NORMALIZER
    qc = chunks(q, GLA_DK) * (GLA_DK ** -0.5)
    kc = chunks(k, GLA_DK)
    vc = chunks(v, GLA_DV)
    gc = chunks(log_decay, GLA_DK)
    causal = jnp.tril(jnp.ones((C, C), dtype=bool))[None, None, :, :, None]

    def step(state, inp):
        q_, k_, v_, g_ = inp
        bcum = jnp.cumsum(g_, axis=2)
        b_last = bcum[:, :, -1:, :]
        decay = jnp.exp(jnp.where(causal, bcum[:, :, :, None, :] - bcum[:, :, None, :, :], -jnp.inf))
        att = jnp.einsum('bhid,bhjd,bhijd->bhij', q_, k_, decay)
        o = (jnp.einsum('bhij,bhje->bhie', att, v_)
             + jnp.einsum('bhid,bhde->bhie', q_ * jnp.exp(bcum), state))
        state = (jnp.exp(b_last[:, :, 0, :])[..., None] * state
                 + jnp.einsum('bhjd,bhje->bhde', k_ * jnp.exp(b_last - bcum), v_))
        return state, o

    state0 = jnp.zeros((B, GLA_HEADS, GLA_DK, GLA_DV), f32)
    _, o = lax.scan(step, state0, (qc, kc, vc, gc))
    o = o.transpose(1, 0, 3, 2, 4).reshape(B, S, GLA_HEADS, GLA_DV)
    o = o * lax.rsqrt(jnp.mean(jnp.square(o), axis=-1, keepdims=True) + RMS_EPS) * norm_g.astype(f32)
    o = o * jax.nn.silu(g.astype(f32).reshape(B, S, GLA_HEADS, GLA_DV))
    return o.reshape(B, S, GROUP_W)


def swa_mixer(q, k, v, sinks):
    B, S, Hq, d = q.shape
    Hkv = k.shape[2]
    G = Hq // Hkv
    NB = S // SWA_BLOCK
    qb = q.reshape(B, NB, SWA_BLOCK, Hkv, G, d)

    def band_keys(t):
        prev = jnp.concatenate([jnp.zeros_like(t[:, :SWA_BLOCK]), t[:, :S - SWA_BLOCK]], axis=1)
        return jnp.concatenate([prev.reshape(B, NB, SWA_BLOCK, Hkv, d),
                                t.reshape(B, NB, SWA_BLOCK, Hkv, d)], axis=2)

    kb, vb = band_keys(k), band_keys(v)
    scores = jnp.einsum('bnqkgd,bnskd->bnkgqs', qb, kb).astype(jnp.float32) * (d ** -0.5)
    qpos = SWA_BLOCK + jnp.arange(SWA_BLOCK)[:, None]
    kpos = jnp.arange(2 * SWA_BLOCK)[None, :]
    rel = qpos - kpos
    band = (rel >= 0) & (rel < SWA_WINDOW)
    first = (jnp.arange(NB) == 0)[:, None, None]
    mask = band[None] & ~(first & (kpos[None] < SWA_BLOCK))
    scores = jnp.where(mask[None, :, None, None], scores, -jnp.inf)
    sink = sinks.astype(jnp.float32).reshape(1, 1, Hkv, G, 1, 1)
    m = jnp.maximum(jnp.max(scores, axis=-1, keepdims=True), sink)
    p = jnp.exp(scores - m)
    probs = p / (jnp.sum(p, axis=-1, keepdims=True) + jnp.exp(sink - m))
    out = jnp.einsum('bnkgqs,bnskd->bnqkgd', probs.astype(v.dtype), vb)
    return out.reshape(B, S, Hq * d)


def moba_mixer(q, k, v):
    B, S, H, d = q.shape
    f32 = jnp.float32
    q, k, v = (t.transpose(0, 2, 1, 3) for t in (q, k, v))
    NB = -(-S // MOBA_BLOCK)
    pad = NB * MOBA_BLOCK - S
    kb = jnp.pad(k, ((0, 0), (0, 0), (0, pad), (0, 0))).reshape(B, H, NB, MOBA_BLOCK, d)
    vb = jnp.pad(v, ((0, 0), (0, 0), (0, pad), (0, 0))).reshape(B, H, NB, MOBA_BLOCK, d)
    kbar = jnp.mean(kb.astype(f32), axis=3)
    own = jnp.arange(S) // MOBA_BLOCK
    gate = jnp.einsum('bhsd,bhnd->bhsn', q.astype(f32), kbar)
    past = jnp.arange(NB)[None, :] < own[:, None]
    gate = jnp.where(past[None, None], gate, jnp.finfo(f32).min)
    ksel = min(MOBA_TOPK, NB)
    _, idx = lax.top_k(gate, ksel)
    sel_valid = jnp.arange(ksel)[None, :] < own[:, None]
    bi = jnp.arange(B)[:, None, None, None]
    hi = jnp.arange(H)[None, :, None, None]
    scale = d ** -0.5
    QC = MOBA_Q_CHUNK

    def chunk(c):
        t0 = c * QC
        qc = lax.dynamic_slice_in_dim(q, t0, QC, axis=2)
        ic = lax.dynamic_slice_in_dim(idx, t0, QC, axis=2)
        sm = lax.dynamic_slice_in_dim(sel_valid, t0, QC, axis=0)
        blk = t0 // MOBA_BLOCK
        k_sel = kb[bi, hi, ic]
        v_sel = vb[bi, hi, ic]
        s_sel = jnp.einsum('bhqd,bhqkpd->bhqkp', qc, k_sel).astype(f32) * scale
        s_sel = jnp.where(sm[None, None, :, :, None], s_sel, -jnp.inf)
        k_own = lax.dynamic_index_in_dim(kb, blk, axis=2, keepdims=False)
        v_own = lax.dynamic_index_in_dim(vb, blk, axis=2, keepdims=False)
        s_own = jnp.einsum('bhqd,bhpd->bhqp', qc, k_own).astype(f32) * scale
        tq = t0 + jnp.arange(QC)
        tk = blk * MOBA_BLOCK + jnp.arange(MOBA_BLOCK)
        s_own = jnp.where((tk[None, :] <= tq[:, None])[None, None], s_own, -jnp.inf)
        s_all = jnp.concatenate([s_sel.reshape(B, H, QC, ksel * MOBA_BLOCK), s_own], axis=-1)
        p = jax.nn.softmax(s_all, axis=-1).astype(v.dtype)
        p_sel = p[..., :ksel * MOBA_BLOCK].reshape(B, H, QC, ksel, MOBA_BLOCK)
        p_own = p[..., ksel * MOBA_BLOCK:]
        return (jnp.einsum('bhqkp,bhqkpd->bhqd', p_sel, v_sel)
                + jnp.einsum('bhqp,bhpd->bhqd', p_own, v_own))

    outs = lax.map(chunk, jnp.arange(S // QC))
    return outs.transpose(1, 0, 3, 2, 4).reshape(B, S, H * d)


def stick_breaking_mixer(q, k, v):
    B, S, H, d = q.shape
    f32 = jnp.float32
    q, k, v = (t.transpose(0, 2, 1, 3) for t in (q, k, v))
    scale = d ** -0.5
    outs = []
    for i in range(S // SB_Q_BLOCK):
        end = (i + 1) * SB_Q_BLOCK
        qb = q[:, :, i * SB_Q_BLOCK:end]
        kb, vb = k[:, :, :end], v[:, :, :end]
        z = jnp.einsum('bhqd,bhsd->bhqs', qb, kb).astype(f32) * scale
        tq = i * SB_Q_BLOCK + jnp.arange(SB_Q_BLOCK)
        ts = jnp.arange(end)
        mask = (ts[None, :] < tq[:, None])[None, None]
        log_keep = jnp.where(mask, jax.nn.log_sigmoid(-z), 0.0)
        log_after = lax.cumsum(log_keep, axis=3, reverse=True) - log_keep
        w = jnp.where(mask, jnp.exp(jax.nn.log_sigmoid(z) + log_after), 0.0)
        outs.append(jnp.einsum('bhqs,bhsd->bhqd', w.astype(v.dtype), vb))
    o = jnp.concatenate(outs, axis=2)
    return o.transpose(0, 2, 1, 3).reshape(B, S, H * d)


def token_mixing(h, cos, sin, w_in, gla_w_gate_up, gla_b_gate_up, gla_norm_g, swa_sinks, w_out):
    B, S, _ = h.shape
    points, acc = [], 0
    for size in IN_SPLIT_SIZES[:-1]:
        acc += size
        points.append(acc)
    (g_q, g_k, g_v, g_g, g_lr, s_q, s_k, s_v,
     m_q, m_k, m_v, b_q, b_k, b_v) = jnp.split(h @ w_in, points, axis=-1)

    y_gla = gla_mixer(g_q, g_k, g_v, g_g, g_lr, gla_w_gate_up, gla_b_gate_up, gla_norm_g)

    s_q = apply_rope(s_q.reshape(B, S, SWA_HEADS, HEAD_DIM), cos, sin)
    s_k = apply_rope(s_k.reshape(B, S, SWA_KV_HEADS, HEAD_DIM), cos, sin)
    y_swa = swa_mixer(s_q, s_k, s_v.reshape(B, S, SWA_KV_HEADS, HEAD_DIM), swa_sinks)

    m_q = apply_rope(m_q.reshape(B, S, MOBA_HEADS, HEAD_DIM), cos, sin)
    m_k = apply_rope(m_k.reshape(B, S, MOBA_HEADS, HEAD_DIM), cos, sin)
    y_moba = moba_mixer(m_q, m_k, m_v.reshape(B, S, MOBA_HEADS, HEAD_DIM))

    y_sb = stick_breaking_mixer(b_q.reshape(B, S, SB_HEADS, HEAD_DIM),
                                b_k.reshape(B, S, SB_HEADS, HEAD_DIM),
                                b_v.reshape(B, S, SB_HEADS, HEAD_DIM))

    y = jnp.concatenate([y_gla.astype(h.dtype), y_swa.astype(h.dtype),
                         y_moba.astype(h.dtype), y_sb.astype(h.dtype)], axis=-1)
    return y @ w_out


def setup_inputs(seed: int = 0) -> dict:
    key = jax.random.key(seed)
    ks = jax.random.split(key, 15)
    f32 = jnp.float32
    L = DEPTH
    nrm = lambda k, shape: jax.random.normal(k, shape, f32)
    x = nrm(ks[0], (BATCH, SEQ, D_MODEL))
    offsets = jax.random.randint(ks[1], (BATCH, 1), 0, 4096, dtype=jnp.int32)
    positions = (offsets + jnp.arange(SEQ, dtype=jnp.int32)[None, :]).astype(jnp.int32)
    w_in = nrm(ks[2], (L, D_MODEL, D_IN)) * D_MODEL ** -0.5
    gla_w_gate_up = nrm(ks[3], (L, GLA_GATE_RANK, GLA_KEY)) * GLA_GATE_RANK ** -0.5
    gla_b_gate_up = 0.1 * nrm(ks[4], (L, GLA_KEY))
    gla_norm_g = 1.0 + 0.02 * nrm(ks[5], (L, GLA_DV))
    swa_sinks = 0.5 * nrm(ks[6], (L, SWA_HEADS))
    w_out = nrm(ks[7], (L, D_MIX, D_MODEL)) * (D_MIX ** -0.5) * DN_BETA
    ffn1_w_gu = nrm(ks[8], (L, D_MODEL, 2 * D_FF)) * D_MODEL ** -0.5
    ffn1_w_down = nrm(ks[9], (L, D_FF, D_MODEL)) * (D_FF ** -0.5) * DN_BETA
    ffn2_w_gu = nrm(ks[10], (L, D_MODEL, 2 * D_FF)) * D_MODEL ** -0.5
    ffn2_w_down = nrm(ks[11], (L, D_FF, D_MODEL)) * (D_FF ** -0.5) * DN_BETA
    ln_g = 1.0 + 0.02 * nrm(ks[12], (L, 3, D_MODEL))
    ln_b = 0.02 * nrm(ks[13], (L, 3, D_MODEL))
    return {"x": x, "positions": positions, "w_in": w_in,
            "gla_w_gate_up": gla_w_gate_up, "gla_b_gate_up": gla_b_gate_up,
            "gla_norm_g": gla_norm_g, "swa_sinks": swa_sinks, "w_out": w_out,
            "ffn1_w_gu": ffn1_w_gu, "ffn1_w_down": ffn1_w_down,
            "ffn2_w_gu": ffn2_w_gu, "ffn2_w_down": ffn2_w_down,
            "ln_g": ln_g, "ln_b": ln_b}


def reference(x, positions, w_in, gla_w_gate_up, gla_b_gate_up, gla_norm_g, swa_sinks, w_out,
              ffn1_w_gu, ffn1_w_down, ffn2_w_gu, ffn2_w_down, ln_g, ln_b):
    cos, sin = rope_tables(positions)
    for l in range(DEPTH):
        x = layer_norm(DN_ALPHA * x + FFN_RES * swiglu(x, ffn1_w_gu[l], ffn1_w_down[l]),
                       ln_g[l, 0], ln_b[l, 0])
        x = layer_norm(DN_ALPHA * x + token_mixing(x, cos, sin, w_in[l], gla_w_gate_up[l],
                                                   gla_b_gate_up[l], gla_norm_g[l],
                                                   swa_sinks[l], w_out[l]),
                       ln_g[l, 1], ln_b[l, 1])
        x = layer_norm(DN_ALPHA * x + FFN_RES * swiglu(x, ffn2_w_gu[l], ffn2_w_down[l]),
                       ln_g[l, 2], ln_b[l, 2])
    return x
```

```python
import math
from contextlib import ExitStack

import numpy as np
import ml_dtypes

import concourse.bass as bass
import concourse.mybir as mybir
from concourse.bass_utils import run_bass_kernel_spmd

F32 = mybir.dt.float32
BF16 = mybir.dt.bfloat16
I32 = mybir.dt.int32
AF = mybir.ActivationFunctionType
ALU = mybir.AluOpType
AX = mybir.AxisListType

D = 4096
KC = D // 128
D_IN = 10768
LN_EPS = 1e-5
RMS_EPS = 1e-5
NEG = -30000.0


class Buf:
    __slots__ = ("name", "last_w", "readers")

    def __init__(self, name):
        self.name = name
        self.last_w = None
        self.readers = []


class Op:
    __slots__ = ("eng", "fn", "deps", "dma", "lane", "sig", "val", "idx")

    def __init__(self, eng, fn, dma, lane):
        self.eng = eng
        self.fn = fn
        self.deps = set()
        self.dma = dma
        self.lane = lane
        self.sig = False
        self.val = None
        self.idx = None


ENGS = ("pe", "act", "dve", "pool", "sp")


class Prog:
    def __init__(self):
        self.ops = []
        self.per_eng = {e: [] for e in ENGS}
        self.barrier_deps = {e: set() for e in ENGS}
        self.last_dma = {}

    def op(self, eng, fn, reads=(), writes=(), dma=False, lane=None, extra=()):
        o = Op(eng, fn, dma, lane)
        o.idx = len(self.ops)
        for x_ in extra:
            if x_ is not None:
                o.deps.add(x_)
        if dma:
            assert lane is not None
        for b in reads:
            if b.last_w is not None:
                o.deps.add(b.last_w)
        for b in writes:
            if b.last_w is not None:
                o.deps.add(b.last_w)
            lastr = {}
            for r in b.readers:
                if r.dma:
                    o.deps.add(r)
                elif r.eng not in lastr or lastr[r.eng].idx < r.idx:
                    lastr[r.eng] = r
            for r in lastr.values():
                o.deps.add(r)
        for b in writes:
            b.last_w = o
            b.readers = []
        for b in reads:
            if b.last_w is not o:
                b.readers.append(o)
        if self.barrier_deps[eng]:
            o.deps |= self.barrier_deps[eng]
            self.barrier_deps[eng] = set()
        o.deps.discard(o)
        self.ops.append(o)
        self.per_eng[eng].append(o)
        if dma:
            self.last_dma[lane] = o
        return o

    def barrier(self):
        pend = set()
        for e in ENGS:
            for o in reversed(self.per_eng[e]):
                if not o.dma:
                    pend.add(o)
                    break
        for o in self.last_dma.values():
            pend.add(o)
        for e in ENGS:
            self.barrier_deps[e] |= pend

    def emit(self, nc, stack, final_ops):
        for o in self.ops:
            for d in o.deps:
                if d.dma:
                    d.sig = True
                elif d.eng == "pe" and o.eng == "pe" and not o.dma:
                    pass
                else:
                    d.sig = True
        for o in final_ops:
            o.sig = True
        eng_sem = {e: stack.enter_context(nc.semaphore("s_" + e)) for e in ENGS}
        lane_sem = {}
        cnt = {e: 0 for e in ENGS}
        lane_cnt = {}
        for o in self.ops:
            if not o.sig:
                continue
            if o.dma:
                if o.lane not in lane_sem:
                    lane_sem[o.lane] = stack.enter_context(
                        nc.semaphore("l_%d" % len(lane_sem)))
                    lane_cnt[o.lane] = 0
                lane_cnt[o.lane] += 16
                o.val = (lane_sem[o.lane], lane_cnt[o.lane])
            else:
                cnt[o.eng] += 1
                o.val = (eng_sem[o.eng], cnt[o.eng])
        self.stats = dict(n_ops=len(self.ops), sig=dict(cnt), lanes=len(lane_sem),
                          per_eng={e: len(v) for e, v in self.per_eng.items()})
        block = stack.enter_context(nc.Block())

        def run(eng_name, handle):
            seen = {}
            for o in self.per_eng[eng_name]:
                need = {}
                for d in o.deps:
                    if (not d.dma) and d.eng == "pe" and eng_name == "pe" and not o.dma:
                        continue
                    sem, v = d.val
                    k = id(sem)
                    if seen.get(k, 0) < v and need.get(k, (None, 0))[1] < v:
                        need[k] = (sem, v)
                for k, (sem, v) in need.items():
                    handle.wait_ge(sem, v)
                    seen[k] = v
                ins = o.fn(handle)
                if o.sig:
                    sem, v = o.val
                    ins.then_inc(sem, 16 if o.dma else 1)
            for o in final_ops:
                if eng_name == "sp":
                    sem, v = o.val
                    if seen.get(id(sem), 0) < v:
                        handle.wait_ge(sem, v)
                        seen[id(sem)] = v

        @block.tensor
        def _(h):
            run("pe", h)

        @block.scalar
        def _(h):
            run("act", h)

        @block.vector
        def _(h):
            run("dve", h)

        @block.gpsimd
        def _(h):
            run("pool", h)

        @block.sync
        def _(h):
            run("sp", h)


class Ring:
    def __init__(self, slots):
        self.slots = slots
        self.i = 0

    def next(self):
        s = self.slots[self.i % len(self.slots)]
        self.i += 1
        return s


class Cfg:
    def __init__(self, S=2048, DFF=11008, L=2, T=512):
        self.S, self.DFF, self.L, self.T = S, DFF, L, T
        self.NF = DFF // 128
        self.NT = S // T
        self.alpha = float((2 * L) ** 0.25)


def host_consts():
    c = {}
    c["ident_f"] = np.eye(128, dtype=np.float32)
    c["ones_b"] = np.ones((128, 128), dtype=ml_dtypes.bfloat16)
    c["ident_b"] = np.eye(128, dtype=ml_dtypes.bfloat16)
    perm = np.zeros((128, 128), np.float32)
    for m in range(128):
        perm[(m + 64) % 128, m] = 1.0
    c["perm_f"] = perm
    inv = 1.0 / (10000.0 ** (np.arange(0, 128, 2, dtype=np.float32) / np.float32(128)))
    c["inv2"] = np.concatenate([inv, inv]).astype(np.float32).reshape(128, 1)
    sgn = np.concatenate([-np.ones(64), np.ones(64)]).astype(np.float32).reshape(128, 1)
    c["sgn2"] = sgn
    tk = np.arange(128)[:, None]
    tq = np.arange(128)[None, :]
    cur = np.where(tk <= tq, 0.0, NEG).astype(np.float32)
    prv = np.where(tk > tq, 0.0, NEG).astype(np.float32)
    c["mask_cur4"] = np.tile(cur, (1, 4)).astype(ml_dtypes.bfloat16)
    c["mask_prev4"] = np.tile(prv, (1, 4)).astype(ml_dtypes.bfloat16)
    i = np.arange(128)[:, None]
    j = np.arange(512)[None, :]
    msb = [(j > m * 128 + i).astype(np.float32) for m in range(4)] + [np.ones((128, 512), np.float32)]
    c["msb5"] = np.concatenate(msb, axis=1)
    a = np.arange(128)
    c["ustrict_f"] = (a[:, None] > a[None, :]).astype(np.float32)
    c["ones_f"] = np.ones((128, 128), np.float32)
    e8 = np.zeros((8, 8 * 128), np.float32)
    for n in range(8):
        e8[n, n * 128:(n + 1) * 128] = 1.0
    c["e8"] = e8.astype(ml_dtypes.bfloat16)
    pm = np.zeros((128, 64), np.float32)
    for own in range(8):
        for n in range(8):
            pm[:, own * 8 + n] = 0.0 if n < own else -1e30
    c["pastmask"] = pm
    jq = np.arange(256)[None, :]
    mown = [np.where(i + m * 128 <= jq, 0.0, NEG).astype(np.float32) for m in range(2)]
    c["mown"] = np.concatenate(mown, axis=1).astype(ml_dtypes.bfloat16)
    c["tn_f"] = np.where(a[:, None] <= a[None, :], -1.0 / 16, 0.0).astype(np.float32)
    c["un_f"] = np.where(a[:, None] > a[None, :], -1.0 / 16, 0.0).astype(np.float32)
    c["mtri01"] = (a[:, None] <= a[None, :]).astype(np.float32)
    return c


def build(cfg, phases=("ffn1", "mix", "ffn2"), ZERO_Y_ROWS=(), MIXERS=("gla", "swa", "moba", "sb"), PRECAST=True):
    S, T, NF, NT, L = cfg.S, cfg.T, cfg.NF, cfg.NT, cfg.L
    alpha = cfg.alpha
    nc = bass.Bass("TRN2", target_bir_lowering=False)
    P = Prog()
    st = ExitStack()

    def din(name, shape, dt=F32):
        return nc.dram_tensor(name, list(shape), dt, kind="ExternalInput").ap()

    def dscr(name, shape, dt=F32):
        return nc.dram_tensor(name, list(shape), dt, kind="Internal").ap()

    x_in = din("x", [S, D])
    w_gu = [din("ffn1_w_gu", [L * D, 2 * cfg.DFF]), din("ffn2_w_gu", [L * D, 2 * cfg.DFF])]
    w_dn = [din("ffn1_w_down", [L * cfg.DFF, D]), din("ffn2_w_down", [L * cfg.DFF, D])]
    ln_g = din("ln_g", [L * 3, D])
    ln_b = din("ln_b", [L * 3, D])
    pos_in = nc.dram_tensor("positions", [1, S], I32, kind="ExternalInput").ap()
    w_in = din("w_in", [L * D, D_IN])
    w_out = din("w_out", [L * D, D])
    gla_wg = din("gla_w_gate_up", [L * 16, 512])
    gla_bg = din("gla_b_gate_up", [L, 512])
    gla_ng = din("gla_norm_g", [L, 256])
    sinks_in = din("swa_sinks", [L, 8])
    c_ident_b = din("ident_b", [128, 128], BF16)
    c_perm_f = din("perm_f", [128, 128])
    c_inv2 = din("inv2", [128, 1])
    c_sgn2 = din("sgn2", [128, 1])
    c_mcur = din("mask_cur4", [128, 512], BF16)
    c_mprev = din("mask_prev4", [128, 512], BF16)
    featT = dscr("featT", [42 * 128, S], BF16)
    tokM = dscr("tokM", [S, 3328], BF16)
    yT = dscr("yT", [D, S], BF16)
    b_feat = Buf("featT")
    b_tok = Buf("tokM")
    b_yT = Buf("yT")
    c_msb5 = din("msb5", [128, 5 * 512])
    c_ustrict = din("ustrict_f", [128, 128])
    c_ones_f = din("ones_f", [128, 128])
    c_e8 = din("e8", [8, 1024], BF16)
    c_pastmask = din("pastmask", [128, 64])
    c_mown = din("mown", [128, 512], BF16)
    c_tn = din("tn_f", [128, 128])
    c_un = din("un_f", [128, 128])
    c_mtri = din("mtri01", [128, 128])
    gfT = dscr("gfT", [16 * 128, S])
    glrT = dscr("glrT", [16, S])
    gktok = dscr("gktok", [S, 512])
    b_gf = Buf("gfT")
    b_glr = Buf("glrT")
    b_gkt = Buf("gktok")
    c_ident_f = din("ident_f", [128, 128])
    c_ones_b = din("ones_b", [128, 128], BF16)
    out = nc.dram_tensor("out", [S, D], F32, kind="ExternalOutput").ap()

    zs = [dscr("zA", [D, S]), dscr("zB", [D, S])]
    xaT = dscr("xaT", [D, S])
    zbuf = [[Buf("z%d_%d" % (i, t)) for t in range(NT)] for i in range(2)]
    xabuf = [Buf("xa_%d" % t) for t in range(NT)]

    def sb(name, shape, dt):
        return st.enter_context(nc.sbuf_tensor(name, list(shape), dt))

    def ps(name, shape, dt=F32):
        return st.enter_context(nc.psum_tensor(name, list(shape), dt))

    ident_f = sb("ident_f_sb", [128, 128], F32)
    ones_b = sb("ones_b_sb", [128, 128], BF16)
    lng = sb("lng", [128, L * 3, KC], F32)
    lnb = sb("lnb", [128, L * 3, KC], F32)
    lnga = sb("lnga", [128, L * 3, KC], F32)
    lnba = sb("lnba", [128, L * 3, KC], F32)
    ident_b = sb("ident_b_sb", [128, 128], BF16)
    perm_f = sb("perm_f_sb", [128, 128], F32)
    inv2 = sb("inv2_sb", [128, 1], F32)
    sgn2 = sb("sgn2_sb", [128, 1], F32)
    mcur = sb("mcur_sb", [128, 512], BF16)
    mprev = sb("mprev_sb", [128, 512], BF16)
    mean_bc = sb("mean_bc", [128, S], F32)
    rstd_bc = sb("rstd_bc", [128, S], F32)
    b_const = Buf("consts")
    b_stats = [Buf("stats_%d" % t) for t in range(NT)]

    xT = sb("xT", [128, KC, T], BF16)
    b_xT = Buf("xT")
    ARENA = max(NF * T, 40 * 1024)
    arena = sb("arena", [128, ARENA], BF16)
    AT = arena[:, 0:NF * T].rearrange("p (f t) -> p f t", f=NF)
    b_AT = [Buf("AT%d" % f) for f in range(NF)]
    _aoff = [0]

    def carve_reset():
        _aoff[0] = 0

    def carve(shape, dt):
        n = 1
        for s_ in shape[1:]:
            n *= s_
        nb = n * (4 if dt == F32 else 2)
        nb = (nb + 63) // 64 * 64
        o = _aoff[0]
        _aoff[0] += nb // 2
        assert _aoff[0] <= ARENA, ("arena overflow", _aoff[0], ARENA)
        v = arena[:, o:o + nb // 2]
        if dt == F32:
            v = v.bitcast(F32)
        v = v[0:shape[0], 0:n]
        if len(shape) == 3:
            v = v.rearrange("p (a b) -> p a b", a=shape[1])
        elif len(shape) == 4:
            v = v.rearrange("p (a b c) -> p a b c", a=shape[1], b=shape[2])
        return v
    NRING = 3
    ring_t = [sb("wr%d" % i, [128, 16 * 512], BF16) for i in range(NRING)]
    ring = Ring([(ring_t[i], (Buf("wr%da" % i), Buf("wr%db" % i)), "wr%d" % i) for i in range(NRING)])
    ld_t = [sb("ld%d" % i, [128, T], F32) for i in range(2)]
    ldr = Ring([(ld_t[i], Buf("ld%d" % i), "ld%d" % i) for i in range(2)])
    st_t = [sb("stg%d" % i, [128, T], F32) for i in range(2)]
    stg = Ring([(st_t[i], Buf("stg%d" % i), "stg%d" % i) for i in range(2)])
    sg_t = [sb("sg%d" % i, [128, T], F32) for i in range(2)]
    sgr = Ring([(sg_t[i], Buf("sg%d" % i), "sg%d" % i) for i in range(2)])
    zb_t = [sb("zb%d" % i, [128, 2, T], BF16) for i in range(2)]
    zbr = Ring([(zb_t[i], Buf("zbq%d" % i), None) for i in range(2)])
    tmp_f = sb("tmp_f", [128, T], F32)
    b_tmp = Buf("tmp_f")

    pbank = [ps("pb%d" % i, [128, 512]) for i in range(8)]
    b_pb = [Buf("pb%d" % i) for i in range(8)]

    P.op("sp", lambda e: e.dma_start(out=ident_f[:], in_=c_ident_f), writes=[b_const], dma=True, lane="c0")
    P.op("sp", lambda e: e.dma_start(out=ones_b[:], in_=c_ones_b), writes=[b_const], dma=True, lane="c1")
    P.op("sp", lambda e: e.dma_start(out=lng[:], in_=ln_g.rearrange("j (k p) -> p j k", p=128),
                                     allow_slow_non_contiguous=True),
         writes=[b_const], dma=True, lane="c2")
    P.op("sp", lambda e: e.dma_start(out=lnb[:], in_=ln_b.rearrange("j (k p) -> p j k", p=128),
                                     allow_slow_non_contiguous=True),
         writes=[b_const], dma=True, lane="c3")
    for i_, (dst_, src_) in enumerate(((ident_b, c_ident_b), (perm_f, c_perm_f), (inv2, c_inv2), (sgn2, c_sgn2),
                                       (mcur, c_mcur), (mprev, c_mprev))):
        P.op("sp", lambda e, dst_=dst_, src_=src_: e.dma_start(out=dst_[:], in_=src_), writes=[b_const],
             dma=True, lane="c%d" % (4 + i_))
    P.op("dve", lambda e: e.tensor_scalar(out=lnga[:], in0=lng[:], scalar1=alpha, scalar2=None, op0=ALU.mult),
         reads=[b_const], writes=[b_const])
    P.op("dve", lambda e: e.tensor_scalar(out=lnba[:], in0=lnb[:], scalar1=alpha, scalar2=None, op0=ALU.mult),
         reads=[b_const], writes=[b_const])

    carve_reset()
    xr_t = [carve([128, D], F32)]
    b_xr = Buf("xr")
    tr_t = carve([128, KC, 128], F32)
    b_tr = Buf("trs")
    zA_v = zs[0].rearrange("(k p) s -> p k s", p=128)
    for tcn in range(S // 128):
        t0 = tcn * 128
        tt = t0 // T
        P.op("sp", lambda e, t0=t0: e.dma_start(out=xr_t[0], in_=x_in[t0:t0 + 128, :]),
             writes=[b_xr], dma=True, lane="xr")
        for g in range(KC // 4):
            bk = g % 2
            for j in range(4):
                kc = g * 4 + j
                P.op("pe", lambda e, kc=kc, j=j, bk=bk: e.transpose(
                    out=pbank[bk][:, j * 128:(j + 1) * 128], in_=xr_t[0][:, kc * 128:(kc + 1) * 128],
                    identity=ident_f[:]), reads=[b_xr, b_const], writes=[b_pb[bk]])
            P.op("act" if g % 2 else "dve",
                 (lambda e, g=g, bk=bk: e.activation(
                     out=tr_t[:, g * 4:(g + 1) * 4, :],
                     in_=pbank[bk][:].rearrange("p (j t) -> p j t", j=4), func=AF.Copy)) if g % 2 else
                 (lambda e, g=g, bk=bk: e.tensor_copy(
                     out=tr_t[:, g * 4:(g + 1) * 4, :],
                     in_=pbank[bk][:].rearrange("p (j t) -> p j t", j=4))),
                 reads=[b_pb[bk]], writes=[b_tr])
        P.op("sp", lambda e, t0=t0: e.dma_start(out=zA_v[:, :, t0:t0 + 128], in_=tr_t),
             reads=[b_tr], writes=[zbuf[0][tt]], dma=True, lane="trs")

    P.barrier()

    def load_stage(zi, tt, ln, kcs=None, deep=False):
        t0 = tt * T
        kcs = list(range(KC)) if kcs is None else list(kcs)
        slots = [ldr.slots[0], ldr.slots[1]] + ([sgr.slots[0], sgr.slots[1]] if deep else [])
        npre = len(slots) - 1
        loaded = {}

        def issue_load(i):
            kc = kcs[i]
            lt, lb, ll = slots[i % len(slots)]
            P.op(AQ[0], lambda e, kc=kc, lt=lt: e.dma_start(
                out=lt[:], in_=zs[zi][kc * 128:(kc + 1) * 128, t0:t0 + T]),
                reads=[zbuf[zi][tt]], writes=[lb], dma=True, lane=ll)
            loaded[i] = (lt, lb)

        for i in range(min(npre, len(kcs))):
            issue_load(i)
        for i, kc in enumerate(kcs):
            if i + npre < len(kcs):
                issue_load(i + npre)
            lt, lb = loaded.pop(i)
            sx, sxb, sl = stg.next()
            if ln is None:
                P.op("act", lambda e, kc=kc, lt=lt: e.activation(out=xT[:, kc, :], in_=lt[:], func=AF.Copy),
                     reads=[lb], writes=[b_xT])
                P.op("dve", lambda e, lt=lt, sx=sx: e.tensor_scalar(
                    out=sx[:], in0=lt[:], scalar1=alpha, scalar2=None, op0=ALU.mult),
                    reads=[lb], writes=[sxb])
            else:
                P.op("dve", lambda e, lt=lt: e.tensor_tensor(
                    out=lt[:], in0=lt[:], in1=mean_bc[:, t0:t0 + T], op=ALU.subtract),
                    reads=[lb, b_stats[tt]], writes=[lb])
                P.op("dve", lambda e, lt=lt: e.tensor_tensor(
                    out=lt[:], in0=lt[:], in1=rstd_bc[:, t0:t0 + T], op=ALU.mult),
                    reads=[lb, b_stats[tt]], writes=[lb])
                P.op("act", lambda e, kc=kc, lt=lt: e.activation(
                    out=xT[:, kc, :], in_=lt[:], func=AF.Identity,
                    scale=lng[:, ln, kc:kc + 1], bias=lnb[:, ln, kc:kc + 1]),
                    reads=[lb, b_const], writes=[b_xT])
                P.op("act", lambda e, kc=kc, lt=lt, sx=sx: e.activation(
                    out=sx[:], in_=lt[:], func=AF.Identity,
                    scale=lnga[:, ln, kc:kc + 1], bias=lnba[:, ln, kc:kc + 1]),
                    reads=[lb, b_const], writes=[sxb])
            P.op(AQ[0], lambda e, kc=kc, sx=sx: e.dma_start(
                out=xaT[kc * 128:(kc + 1) * 128, t0:t0 + T], in_=sx[:]),
                reads=[sxb], writes=[xabuf[tt]], dma=True, lane=sl)

    def epilogue_prefetch(tt, dc):
        t0 = tt * T
        lt, lb, ll = sgr.next()
        P.op(AQ[0], lambda e: e.dma_start(out=lt[:], in_=xaT[dc * 128:(dc + 1) * 128, t0:t0 + T]),
             reads=[xabuf[tt]], writes=[lb], dma=True, lane=ll)
        return lt, lb

    def epilogue(zo, tt, dc, pb_i, res_scale, pre):
        t0 = tt * T
        lt, lb = pre
        sx, sxb, sl = stg.next()
        P.op("dve", lambda e: e.scalar_tensor_tensor(
            out=sx[:], in0=pbank[pb_i][:, 0:T], scalar=res_scale, in1=lt[:], op0=ALU.mult, op1=ALU.add),
            reads=[b_pb[pb_i], lb], writes=[sxb])
        P.op(AQ[0], lambda e: e.dma_start(out=zs[zo][dc * 128:(dc + 1) * 128, t0:t0 + T], in_=sx[:]),
             reads=[sxb], writes=[zbuf[zo][tt]], dma=True, lane=sl)
        zq, zqb, _ = zbr.next()
        P.op("act", lambda e: e.activation(out=zq[:, 0, :], in_=sx[:], func=AF.Copy),
             reads=[sxb], writes=[zqb])
        P.op("act", lambda e: e.activation(out=zq[:, 1, :], in_=sx[:], func=AF.Square),
             reads=[sxb], writes=[zqb])
        P.op("pe", lambda e: e.matmul(pbank[6][:, 0:T], lhsT=ones_b[:], rhs=zq[:, 0, :],
                                      start=(dc == 0), stop=(dc == KC - 1)),
             reads=[zqb, b_const], writes=[b_pb[6]])
        P.op("pe", lambda e: e.matmul(pbank[7][:, 0:T], lhsT=ones_b[:], rhs=zq[:, 1, :],
                                      start=(dc == 0), stop=(dc == KC - 1)),
             reads=[zqb, b_const], writes=[b_pb[7]])
        if dc == KC - 1:
            P.op("act", lambda e: e.activation(out=mean_bc[:, t0:t0 + T], in_=pbank[6][:, 0:T],
                                               func=AF.Copy, scale=1.0 / D),
                 reads=[b_pb[6]], writes=[b_stats[tt]])
            P.op("dve", lambda e: e.tensor_tensor(out=tmp_f[:], in0=mean_bc[:, t0:t0 + T],
                                                  in1=mean_bc[:, t0:t0 + T], op=ALU.mult),
                 reads=[b_stats[tt]], writes=[b_tmp])
            P.op("dve", lambda e: e.scalar_tensor_tensor(
                out=tmp_f[:], in0=pbank[7][:, 0:T], scalar=1.0 / D, in1=tmp_f[:],
                op0=ALU.mult, op1=ALU.subtract), reads=[b_pb[7], b_tmp], writes=[b_tmp])
            P.op("dve", lambda e: e.tensor_scalar(out=tmp_f[:], in0=tmp_f[:], scalar1=LN_EPS, scalar2=None,
                                                  op0=ALU.add), reads=[b_tmp], writes=[b_tmp])
            P.op("act", lambda e: e.activation(out=tmp_f[:], in_=tmp_f[:], func=AF.Sqrt),
                 reads=[b_tmp], writes=[b_tmp])
            P.op("dve", lambda e: e.reciprocal(out=rstd_bc[:, t0:t0 + T], in_=tmp_f[:]),
                 reads=[b_tmp], writes=[b_stats[tt]])

    AQ = ["sp"]

    class Pacer:
        def __init__(self):
            self.q = []
            self.lane = None
            self.rate = 0.0
            self.acc = 0.0

        def load(self, pc, nsteps):
            self.flush()
            self.q = list(pc["ops"])
            self.lane = pc["lane"]
            self.rate = len(self.q) / max(1.0, 0.8 * nsteps)
            self.acc = 0.0

        def tick(self, after=None):
            self.acc += self.rate
            while self.acc >= 1.0 and self.q:
                self.acc -= 1.0
                fn, b = self.q.pop(0)
                P.op("pool", fn, writes=[b], dma=True, lane=self.lane, extra=[after])

        def flush(self):
            while self.q:
                fn, b = self.q.pop(0)
                P.op("pool", fn, writes=[b], dma=True, lane=self.lane)

    pacer = Pacer()
    DG = D // 256

    def mk_precast_ffn(l, which):
        gu_b = dscr("gub_%d_%d" % (l, which), [NF * 128, KC * 256], BF16)
        dn_b = dscr("dnb_%d_%d" % (l, which), [DG * 128, NF * 256], BF16)
        wg = w_gu[which][l * D:(l + 1) * D, :].rearrange("(k p) f -> p k f", p=128)
        wd = w_dn[which][l * cfg.DFF:(l + 1) * cfg.DFF, :].rearrange("(c p) d -> p c d", p=128)
        ops, bufs = [], []
        for fc in range(NF):
            for half in (0, 1):
                b = Buf("pc")
                bufs.append(b)
                ops.append((lambda e, fc=fc, half=half: e.dma_start(
                    out=gu_b[fc * 128:(fc + 1) * 128, :].rearrange("p (k c) -> p k c", k=KC)[:, :, half * 128:(half + 1) * 128],
                    in_=wg[:, :, half * cfg.DFF + fc * 128:half * cfg.DFF + (fc + 1) * 128]), b))
        for dg in range(DG):
            b = Buf("pc")
            bufs.append(b)
            ops.append((lambda e, dg=dg: e.dma_start(
                out=dn_b[dg * 128:(dg + 1) * 128, :].rearrange("p (c d) -> p c d", c=NF),
                in_=wd[:, :, dg * 256:(dg + 1) * 256]), b))
        return dict(gu_b=gu_b, dn_b=dn_b, ops=ops, bufs=bufs, lane="pcf%d%d" % (l, which))

    FFN_STEPS = NT * (NF + DG * ((NF + 31) // 32))

    def ffn(l, which, zi, zo, ln, pc=None):
        wg = w_gu[which][l * D:(l + 1) * D, :].rearrange("(k p) f -> p k f", p=128)
        wd = w_dn[which][l * cfg.DFF:(l + 1) * cfg.DFF, :].rearrange("(c p) d -> p c d", p=128)
        AQ[0] = "sp" if pc is None else "pool"
        first = [True]

        def pcreads():
            if pc is not None and first[0]:
                first[0] = False
                return list(pc["bufs"])
            return []

        load_stage(zi, 0, ln, deep=True)
        for tt in range(NT):
            for fc in range(NF):
                wt, wbs, wl = ring.next()
                wv = wt[:, 0:KC * 256].rearrange("p (k c) -> p k c", k=KC)
                if pc is None:
                    P.op("pool", lambda e, fc=fc, wv=wv: e.dma_start(
                        out=wv[:, :, 0:128], in_=wg[:, :, fc * 128:(fc + 1) * 128]),
                        writes=[wbs[0]], dma=True, lane=wl + "g")
                    lop = P.op("pool", lambda e, fc=fc, wv=wv: e.dma_start(
                        out=wv[:, :, 128:256], in_=wg[:, :, cfg.DFF + fc * 128:cfg.DFF + (fc + 1) * 128]),
                        writes=[wbs[1]], dma=True, lane=wl + "u")
                else:
                    lop = P.op("sp", lambda e, fc=fc, wt=wt: e.dma_start(
                        out=wt[:, 0:KC * 256], in_=pc["gu_b"][fc * 128:(fc + 1) * 128, :]),
                        reads=pcreads(), writes=[wbs[0], wbs[1]], dma=True, lane=wl)
                pacer.tick(last_mm[0])
                pg, pu = (0, 1) if fc % 2 == 0 else (2, 3)
                for half, pbi in ((0, pg), (1, pu)):
                    for kc in range(KC):
                        last_mm[0] = P.op("pe", lambda e, kc=kc, wv=wv, half=half, pbi=pbi: e.matmul(
                            pbank[pbi][:, 0:T], lhsT=wv[:, kc, half * 128:(half + 1) * 128], rhs=xT[:, kc, :],
                            start=(kc == 0), stop=(kc == KC - 1)),
                            reads=[wbs[half], b_xT], writes=[b_pb[pbi]])
                sgt, sgb, _ = sgr.next()
                P.op("act", lambda e, sgt=sgt, pg=pg: e.activation(out=sgt[:], in_=pbank[pg][:, 0:T], func=AF.Silu),
                     reads=[b_pb[pg]], writes=[sgb])
                P.op("dve", lambda e, sgt=sgt, pu=pu, fc=fc: e.tensor_tensor(
                    out=AT[:, fc, :], in0=sgt[:], in1=pbank[pu][:, 0:T], op=ALU.mult),
                    reads=[sgb, b_pb[pu]], writes=[b_AT[fc]])
            dview = wd if pc is None else None
            per = KC // DG

            def between(dg, tt=tt):
                if tt + 1 < NT:
                    load_stage(zi, tt + 1, ln, kcs=range(dg * per, (dg + 1) * per))

            down_like(AT, b_AT, NF, dview, zo, tt, 0.5,
                      pcsrc=None if pc is None else pc["dn_b"], pcreads=pcreads, between=between)

    TWO_PI = 2.0 * math.pi
    SCALE = 128 ** -0.5

    last_mm = [None]

    def down_like(src_t, src_bufs, nch, wview, zo, tt, res_scale, pcsrc=None, pcreads=None, between=None):
        for dg in range(D // 256):
            pres = [epilogue_prefetch(tt, dg * 2 + j) for j in range(2)]
            c0 = 0
            while c0 < nch:
                c1 = min(nch, c0 + 32)
                wt, wbs, wl = ring.next()
                wv = wt[:, 0:32 * 256].rearrange("p (c d) -> p c d", c=32)
                if pcsrc is None:
                    P.op("pool", lambda e, wv=wv, c0=c0, c1=c1, dg=dg: e.dma_start(
                        out=wv[:, 0:c1 - c0, :], in_=wview[:, c0:c1, dg * 256:(dg + 1) * 256]),
                        reads=[wbs[1]], writes=[wbs[0], wbs[1]], dma=True, lane=wl)
                else:
                    P.op("sp", lambda e, wv=wv, c0=c0, c1=c1, dg=dg: e.dma_start(
                        out=wv[:, 0:c1 - c0, :],
                        in_=pcsrc[dg * 128:(dg + 1) * 128, :].rearrange("p (c d) -> p c d", c=nch)[:, c0:c1, :]),
                        reads=[wbs[1]] + pcreads(), writes=[wbs[0], wbs[1]], dma=True, lane=wl)
                pacer.tick(last_mm[0])
                for c in range(c0, c1):
                    for j in range(2):
                        last_mm[0] = P.op("pe", lambda e, wv=wv, c=c, c0=c0, j=j: e.matmul(
                            pbank[4 + j][:, 0:T], lhsT=wv[:, c - c0, j * 128:(j + 1) * 128], rhs=src_t[:, c, :],
                            start=(c == 0), stop=(c == nch - 1)),
                            reads=[wbs[0], src_bufs[c]], writes=[b_pb[4 + j]])
                c0 = c1
            for j in range(2):
                epilogue(zo, tt, dg * 2 + j, 4 + j, res_scale, pres[j])
            if between is not None:
                between(dg)

    def mix_chunks():
        feat_chunks = []
        for i in range(8):
            feat_chunks.append((3088 + i * 128, "rope", i))
        for i in range(2):
            feat_chunks.append((4112 + i * 128, "rope", 8 + i))
        for i in range(8):
            feat_chunks.append((5648 + i * 128, "rope", 18 + i))
        for i in range(8):
            feat_chunks.append((4624 + i * 128, "rope", 10 + i))
        for i in range(8):
            feat_chunks.append((7696 + i * 128, "plain", 26 + i))
        for i in range(8):
            feat_chunks.append((8720 + i * 128, "plain", 34 + i))
        for i in range(4):
            feat_chunks.append((0 + i * 128, "f32", i))
        for i in range(4):
            feat_chunks.append((512 + i * 128, "f32", 4 + i))
        for i in range(8):
            feat_chunks.append((2048 + i * 128, "f32", 8 + i))
        feat_chunks.append((3072, "lr", 0))
        tok_groups = []
        for i in range(4):
            tok_groups.append((1024 + i * 256, i * 256))
        tok_groups.append((4368, 1024))
        for i in range(4):
            tok_groups.append((6672 + i * 256, 1280 + i * 256))
        for i in range(4):
            tok_groups.append((9744 + i * 256, 2304 + i * 256))
        return feat_chunks, tok_groups

    MIX_STEPS = NT * (60 + 15 + DG)

    def mk_precast_mix(l):
        feat_chunks, tok_groups = mix_chunks()
        nfc = len(feat_chunks)
        winf_b = dscr("winf_%d" % l, [nfc * 128, KC * 128], BF16)
        wint_b = dscr("wint_%d" % l, [15 * 128, KC * 256], BF16)
        wo_b = dscr("wo_%d" % l, [DG * 128, KC * 256], BF16)
        win_v = w_in[l * D:(l + 1) * D, :].rearrange("(k p) f -> p k f", p=128)
        wout_v = w_out[l * D:(l + 1) * D, :].rearrange("(c p) d -> p c d", p=128)
        ops, bufs = [], []
        for i, (col0, kind, drow) in enumerate(feat_chunks):
            ncol = 16 if kind == "lr" else 128
            b = Buf("pc")
            bufs.append(b)
            ops.append((lambda e, i=i, col0=col0, ncol=ncol: e.dma_start(
                out=winf_b[i * 128:(i + 1) * 128, 0:KC * ncol].rearrange("p (k c) -> p k c", k=KC),
                in_=win_v[:, :, col0:col0 + ncol]), b))
        tcols = [c for (c, _) in tok_groups] + [512, 768]
        for j, col0 in enumerate(tcols):
            b = Buf("pc")
            bufs.append(b)
            ops.append((lambda e, j=j, col0=col0: e.dma_start(
                out=wint_b[j * 128:(j + 1) * 128, :].rearrange("p (k c) -> p k c", k=KC),
                in_=win_v[:, :, col0:col0 + 256]), b))
        for dg in range(DG):
            b = Buf("pc")
            bufs.append(b)
            ops.append((lambda e, dg=dg: e.dma_start(
                out=wo_b[dg * 128:(dg + 1) * 128, :].rearrange("p (c d) -> p c d", c=KC),
                in_=wout_v[:, :, dg * 256:(dg + 1) * 256]), b))
        return dict(winf_b=winf_b, wint_b=wint_b, wo_b=wo_b, ops=ops, bufs=bufs, lane="pcm%d" % l)

    def mixer(l, zi, zo, ln, pc=None):
        P.barrier()
        AQ[0] = "sp" if pc is None else "pool"
        first = [True]

        def pcreads():
            if pc is not None and first[0]:
                first[0] = False
                return list(pc["bufs"])
            return []

        carve_reset()
        cos2 = carve([128, S], F32)
        sinS = carve([128, S], F32)
        rtmp = carve([128, S], F32)
        rk = carve([128, S], F32)
        qf_t = [carve([128, T], F32) for _ in range(2)]
        qfr = Ring([(qf_t[i], Buf("qf%d" % i), None) for i in range(2)])
        b_tab = Buf("ropetab")
        win_v = w_in[l * D:(l + 1) * D, :].rearrange("(k p) f -> p k f", p=128)
        wout_v = w_out[l * D:(l + 1) * D, :].rearrange("(c p) d -> p c d", p=128)

        posi = rk.bitcast(I32)
        P.op("sp", lambda e: e.dma_start(out=posi, in_=pos_in.partition_broadcast(128)),
             writes=[b_tab], dma=True, lane="tab")
        P.op("dve", lambda e: e.tensor_copy(out=rtmp, in_=posi), reads=[b_tab], writes=[b_tab])
        P.op("dve", lambda e: e.tensor_scalar(out=rtmp, in0=rtmp, scalar1=inv2[:, 0:1], scalar2=None, op0=ALU.mult),
             reads=[b_tab, b_const], writes=[b_tab])

        def reduce_sin(dst, shift):
            ki = rk.bitcast(I32)
            P.op("dve", lambda e: e.tensor_scalar(out=dst, in0=rtmp, scalar1=shift, scalar2=1.0 / TWO_PI,
                                                  op0=ALU.add, op1=ALU.mult), reads=[b_tab], writes=[b_tab])
            P.op("dve", lambda e: e.tensor_copy(out=ki, in_=dst), reads=[b_tab], writes=[b_tab])
            P.op("dve", lambda e: e.tensor_copy(out=dst, in_=ki), reads=[b_tab], writes=[b_tab])
            P.op("dve", lambda e: e.tensor_scalar(out=dst, in0=dst, scalar1=-TWO_PI, scalar2=shift,
                                                  op0=ALU.mult, op1=ALU.add), reads=[b_tab], writes=[b_tab])
            P.op("dve", lambda e: e.tensor_tensor(out=dst, in0=dst, in1=rtmp, op=ALU.add),
                 reads=[b_tab], writes=[b_tab])
            P.op("dve", lambda e: e.tensor_scalar(out=rk, in0=dst, scalar1=-math.pi, scalar2=TWO_PI,
                                                  op0=ALU.is_lt, op1=ALU.mult), reads=[b_tab], writes=[b_tab])
            P.op("dve", lambda e: e.tensor_tensor(out=dst, in0=dst, in1=rk, op=ALU.add),
                 reads=[b_tab], writes=[b_tab])
            P.op("dve", lambda e: e.tensor_scalar(out=rk, in0=dst, scalar1=math.pi, scalar2=-TWO_PI,
                                                  op0=ALU.is_gt, op1=ALU.mult), reads=[b_tab], writes=[b_tab])
            P.op("dve", lambda e: e.tensor_tensor(out=dst, in0=dst, in1=rk, op=ALU.add),
                 reads=[b_tab], writes=[b_tab])
            P.op("dve", lambda e: e.tensor_scalar(out=dst, in0=dst, scalar1=-3.14159, scalar2=3.14159,
                                                  op0=ALU.max, op1=ALU.min), reads=[b_tab], writes=[b_tab])
            P.op("act", lambda e: e.activation(out=dst, in_=dst, func=AF.Sin), reads=[b_tab], writes=[b_tab])

        reduce_sin(cos2, math.pi / 2)
        reduce_sin(sinS, 0.0)
        P.op("dve", lambda e: e.tensor_scalar(out=sinS, in0=sinS, scalar1=sgn2[:, 0:1], scalar2=None, op0=ALU.mult),
             reads=[b_tab, b_const], writes=[b_tab])

        feat_chunks, tok_groups = mix_chunks()
        pbi = 0
        for tt in range(NT):
            t0 = tt * T
            load_stage(zi, tt, ln, deep=True)
            for ci, (col0, kind, drow) in enumerate(feat_chunks):
                wt, wbs, wl = ring.next()
                ncol = 16 if kind == "lr" else 128
                wv = wt[:, 0:KC * ncol].rearrange("p (k c) -> p k c", k=KC)
                if pc is None:
                    lop = P.op("pool", lambda e, wv=wv, col0=col0, ncol=ncol: e.dma_start(
                        out=wv, in_=win_v[:, :, col0:col0 + ncol]),
                        reads=[wbs[1]], writes=[wbs[0], wbs[1]], dma=True, lane=wl)
                else:
                    lop = P.op("sp", lambda e, wt=wt, ci=ci, ncol=ncol: e.dma_start(
                        out=wt[:, 0:KC * ncol], in_=pc["winf_b"][ci * 128:(ci + 1) * 128, 0:KC * ncol]),
                        reads=[wbs[1]] + pcreads(), writes=[wbs[0], wbs[1]], dma=True, lane=wl)
                pacer.tick(last_mm[0])
                pb = pbi % 2
                pbi += 1
                for kc in range(KC):
                    last_mm[0] = P.op("pe", lambda e, kc=kc, wv=wv, pb=pb, ncol=ncol: e.matmul(
                        pbank[pb][0:ncol, 0:T], lhsT=wv[:, kc, :], rhs=xT[:, kc, :], start=(kc == 0), stop=(kc == KC - 1)),
                        reads=[wbs[0], b_xT], writes=[b_pb[pb]])
                if kind in ("f32", "lr"):
                    sx, sxb, sl = stg.next()
                    P.op("act", lambda e, sx=sx, pb=pb, ncol=ncol: e.activation(
                        out=sx[0:ncol, :], in_=pbank[pb][0:ncol, 0:T], func=AF.Copy),
                         reads=[b_pb[pb]], writes=[sxb])
                    if kind == "f32":
                        P.op(AQ[0], lambda e, sx=sx, drow=drow, t0=t0: e.dma_start(
                            out=gfT[drow * 128:(drow + 1) * 128, t0:t0 + T], in_=sx[:]),
                            reads=[sxb], writes=[b_gf], dma=True, lane=sl)
                    else:
                        P.op(AQ[0], lambda e, sx=sx, t0=t0: e.dma_start(out=glrT[:, t0:t0 + T], in_=sx[0:16, :]),
                             reads=[sxb], writes=[b_glr], dma=True, lane=sl)
                    continue
                zq, zqb, _ = zbr.next()
                if kind == "plain":
                    P.op("act", lambda e, zq=zq, pb=pb: e.activation(out=zq[:, 0, :], in_=pbank[pb][:, 0:T], func=AF.Copy),
                         reads=[b_pb[pb]], writes=[zqb])
                else:
                    qf, qfb, _ = qfr.next()
                    P.op("act", lambda e, qf=qf, pb=pb: e.activation(out=qf, in_=pbank[pb][:, 0:T], func=AF.Copy),
                         reads=[b_pb[pb]], writes=[qfb])
                    P.op("pe", lambda e, qf=qf, pb=pb: e.matmul(pbank[2 + pb][:, 0:T], lhsT=perm_f[:], rhs=qf,
                                                                start=True, stop=True),
                         reads=[qfb, b_const], writes=[b_pb[2 + pb]])
                    P.op("dve", lambda e, pb=pb, t0=t0: e.tensor_tensor(out=tmp_f[:], in0=pbank[2 + pb][:, 0:T],
                                                                        in1=sinS[:, t0:t0 + T], op=ALU.mult),
                         reads=[b_pb[2 + pb], b_tab], writes=[b_tmp])
                    P.op("dve", lambda e, qf=qf, t0=t0: e.tensor_tensor(out=qf, in0=qf, in1=cos2[:, t0:t0 + T], op=ALU.mult),
                         reads=[qfb, b_tab], writes=[qfb])
                    P.op("dve", lambda e, qf=qf, zq=zq: e.tensor_tensor(out=zq[:, 0, :], in0=qf, in1=tmp_f[:], op=ALU.add),
                         reads=[qfb, b_tmp], writes=[zqb])
                P.op(AQ[0], lambda e, zq=zq, drow=drow, t0=t0: e.dma_start(
                    out=featT[drow * 128:(drow + 1) * 128, t0:t0 + T], in_=zq[:, 0, :]),
                    reads=[zqb], writes=[b_feat], dma=True, lane="featst")
            for gj, (col0, dcol) in enumerate(tok_groups):
                wt, wbs, wl = ring.next()
                wv = wt[:, 0:KC * 256].rearrange("p (k c) -> p k c", k=KC)
                if pc is None:
                    lop = P.op("pool", lambda e, wv=wv, col0=col0: e.dma_start(out=wv, in_=win_v[:, :, col0:col0 + 256]),
                               reads=[wbs[1]], writes=[wbs[0], wbs[1]], dma=True, lane=wl)
                else:
                    lop = P.op("sp", lambda e, wt=wt, gj=gj: e.dma_start(
                        out=wt[:, 0:KC * 256], in_=pc["wint_b"][gj * 128:(gj + 1) * 128, :]),
                        reads=[wbs[1]] + pcreads(), writes=[wbs[0], wbs[1]], dma=True, lane=wl)
                pacer.tick(last_mm[0])
                for tc4 in range(T // 128):
                    pb = pbi % 2
                    pbi += 1
                    for kc in range(KC):
                        last_mm[0] = P.op("pe", lambda e, kc=kc, wv=wv, pb=pb, tc4=tc4: e.matmul(
                            pbank[pb][:, 0:256], lhsT=xT[:, kc, tc4 * 128:(tc4 + 1) * 128], rhs=wv[:, kc, :],
                            start=(kc == 0), stop=(kc == KC - 1)),
                            reads=[wbs[0], b_xT], writes=[b_pb[pb]])
                    zq, zqb, _ = zbr.next()
                    P.op("act", lambda e, zq=zq, pb=pb: e.activation(out=zq[:, 0, 0:256], in_=pbank[pb][:, 0:256], func=AF.Copy),
                         reads=[b_pb[pb]], writes=[zqb])
                    P.op(AQ[0], lambda e, zq=zq, dcol=dcol, tc4=tc4, t0=t0: e.dma_start(
                        out=tokM[t0 + tc4 * 128:t0 + (tc4 + 1) * 128, dcol:dcol + 256], in_=zq[:, 0, 0:256]),
                        reads=[zqb], writes=[b_tok], dma=True, lane="tokst")

            for gi in range(2):
                col0 = 512 + gi * 256
                wt, wbs, wl = ring.next()
                wv = wt[:, 0:KC * 256].rearrange("p (k c) -> p k c", k=KC)
                if pc is None:
                    lop = P.op("pool", lambda e, wv=wv, col0=col0: e.dma_start(out=wv, in_=win_v[:, :, col0:col0 + 256]),
                               reads=[wbs[1]], writes=[wbs[0], wbs[1]], dma=True, lane=wl)
                else:
                    lop = P.op("sp", lambda e, wt=wt, gi=gi: e.dma_start(
                        out=wt[:, 0:KC * 256], in_=pc["wint_b"][(13 + gi) * 128:(14 + gi) * 128, :]),
                        reads=[wbs[1]] + pcreads(), writes=[wbs[0], wbs[1]], dma=True, lane=wl)
                pacer.tick(last_mm[0])
                for tc4 in range(T // 128):
                    pb = pbi % 2
                    pbi += 1
                    for kc in range(KC):
                        last_mm[0] = P.op("pe", lambda e, kc=kc, wv=wv, pb=pb, tc4=tc4: e.matmul(
                            pbank[pb][:, 0:256], lhsT=xT[:, kc, tc4 * 128:(tc4 + 1) * 128], rhs=wv[:, kc, :],
                            start=(kc == 0), stop=(kc == KC - 1)),
                            reads=[wbs[0], b_xT], writes=[b_pb[pb]])
                    sx, sxb, sl = stg.next()
                    P.op("act", lambda e, sx=sx, pb=pb: e.activation(out=sx[:, 0:256], in_=pbank[pb][:, 0:256], func=AF.Copy),
                         reads=[b_pb[pb]], writes=[sxb])
                    P.op(AQ[0], lambda e, sx=sx, gi=gi, tc4=tc4, t0=t0: e.dma_start(
                        out=gktok[t0 + tc4 * 128:t0 + (tc4 + 1) * 128, gi * 256:(gi + 1) * 256], in_=sx[:, 0:256]),
                        reads=[sxb], writes=[b_gkt], dma=True, lane=sl)

        P.barrier()
        carve_reset()
        esink2 = carve([128, 8], F32)
        b_es = Buf("esink2")
        P.op("sp", lambda e: e.dma_start(out=esink2, in_=sinks_in[l:l + 1, :].partition_broadcast(128)),
             writes=[b_es], dma=True, lane="tab")
        P.op("act", lambda e: e.activation(out=esink2, in_=esink2, func=AF.Exp), reads=[b_es], writes=[b_es])
        zero_t = carve([128, S], BF16)
        b_zero = Buf("zero")
        P.op("dve", lambda e: e.memset(zero_t, 0.0), writes=[b_zero])
        for rows in ZERO_Y_ROWS:
            P.op("sp", lambda e, rows=rows: e.dma_start(out=yT[rows * 128:(rows + 1) * 128, :], in_=zero_t),
                 reads=[b_zero], writes=[b_yT], dma=True, lane="yst")
        featT_v = featT.rearrange("(c p) s -> p c s", p=128)
        tokM_v = tokM.rearrange("(c p) f -> p c f", p=128)
        yT_v = yT.rearrange("(c p) s -> p c s", p=128)
        NQB = S // 128
        kt_t = carve([128, S], BF16)
        b_kt = Buf("swa_kt")
        vt_t = carve([128, NQB, 128], BF16)
        b_vt = Buf("swa_vt")
        q4_t = carve([128, 4, S], BF16)
        b_q4 = Buf("swa_q4")
        pt_t = [carve([128, 512], BF16) for _ in range(4)]
        ptr = Ring([(pt_t[i], Buf("pt%d" % i), None) for i in range(4)])
        den_t = carve([128, 512], F32)
        b_den = Buf("den")
        yo_t = [carve([128, 512], BF16) for _ in range(2)]
        yor = Ring([(yo_t[i], Buf("yo%d" % i), "yo%d" % i) for i in range(2)])
        for kv in range(2):
            P.op("sp", lambda e, kv=kv: e.dma_start(out=kt_t, in_=featT[(8 + kv) * 128:(9 + kv) * 128, :]),
                 reads=[b_feat], writes=[b_kt], dma=True, lane="swak")
            P.op("sp", lambda e, kv=kv: e.dma_start(out=vt_t, in_=tokM_v[:, :, 1024 + kv * 128:1024 + (kv + 1) * 128]),
                 reads=[b_tok], writes=[b_vt], dma=True, lane="swav")
            P.op("sp", lambda e, kv=kv: e.dma_start(out=q4_t, in_=featT_v[:, kv * 4:kv * 4 + 4, :]),
                 reads=[b_feat], writes=[b_q4], dma=True, lane="swaq")
            for n in range(NQB):
                pts = []
                for which in (("cur", n, mcur), ("prev", n - 1, mprev)):
                    nm, kc_, msk = which
                    if kc_ < 0:
                        continue
                    pb = 0 if nm == "cur" else 1
                    P.op("pe", lambda e, kc_=kc_, n=n, pb=pb: e.matmul(
                        pbank[pb][:, 0:512], lhsT=kt_t[:, kc_ * 128:(kc_ + 1) * 128],
                        rhs=q4_t[:, :, n * 128:(n + 1) * 128], start=True, stop=False),
                        reads=[b_kt, b_q4], writes=[b_pb[pb]])
                    P.op("pe", lambda e, msk=msk, pb=pb: e.matmul(
                        pbank[pb][:, 0:512], lhsT=ident_b[:], rhs=msk[:], start=False, stop=True),
                        reads=[b_const], writes=[b_pb[pb]])
                    pt, ptb, _ = ptr.next()
                    P.op("act", lambda e, pt=pt, pb=pb: e.activation(out=pt, in_=pbank[pb][:, 0:512], func=AF.Exp, scale=SCALE),
                         reads=[b_pb[pb]], writes=[ptb])
                    pts.append((pt, ptb, kc_))
                npts = len(pts)
                for i_, (pt, ptb, kc_) in enumerate(pts):
                    P.op("pe", lambda e, pt=pt, kc_=kc_, i_=i_, npts=npts: e.matmul(
                        pbank[2][:, 0:512], lhsT=vt_t[:, kc_, :], rhs=pt, start=(i_ == 0), stop=(i_ == npts - 1)),
                        reads=[ptb, b_vt], writes=[b_pb[2]])
                for i_, (pt, ptb, kc_) in enumerate(pts):
                    P.op("pe", lambda e, pt=pt, i_=i_, npts=npts: e.matmul(
                        pbank[3][:, 0:512], lhsT=ones_b[:], rhs=pt, start=(i_ == 0), stop=(i_ == npts - 1)),
                        reads=[ptb, b_const], writes=[b_pb[3]])
                for g in range(4):
                    P.op("dve", lambda e, g=g, kv=kv: e.tensor_scalar(
                        out=den_t[:, g * 128:(g + 1) * 128], in0=pbank[3][:, g * 128:(g + 1) * 128],
                        scalar1=esink2[:, kv * 4 + g:kv * 4 + g + 1], scalar2=None, op0=ALU.add),
                        reads=[b_pb[3], b_es], writes=[b_den])
                P.op("dve", lambda e: e.reciprocal(out=den_t, in_=den_t), reads=[b_den], writes=[b_den])
                yo, yob, yol = yor.next()
                P.op("dve", lambda e, yo=yo: e.tensor_tensor(out=yo, in0=pbank[2][:, 0:512], in1=den_t, op=ALU.mult),
                     reads=[b_pb[2], b_den], writes=[yob])
                P.op("sp", lambda e, yo=yo, kv=kv, n=n: e.dma_start(
                    out=yT_v[:, 8 + kv * 4:8 + kv * 4 + 4, n * 128:(n + 1) * 128],
                    in_=yo.rearrange("p (g t) -> p g t", g=4)),
                    reads=[yob], writes=[b_yT], dma=True, lane=yol)

        if "sb" in MIXERS:
            P.barrier()
            carve_reset()
            msb = carve([128, 5, 512], F32)
            ustr = carve([128, 128], F32)
            onesf = carve([128, 128], F32)
            b_sc = Buf("sb_consts")
            P.op("sp", lambda e: e.dma_start(out=msb, in_=c_msb5.rearrange("p (m t) -> p m t", m=5)),
                 writes=[b_sc], dma=True, lane="sbc")
            P.op("sp", lambda e: e.dma_start(out=ustr, in_=c_ustrict), writes=[b_sc], dma=True, lane="sbc")
            P.op("sp", lambda e: e.dma_start(out=onesf, in_=c_ones_f), writes=[b_sc], dma=True, lane="sbc")
            qT_s = carve([128, S], BF16)
            kT_s = carve([128, S], BF16)
            v_s = carve([128, NQB, 128], BF16)
            b_qs, b_ks, b_vs = Buf("sb_q"), Buf("sb_k"), Buf("sb_v")
            sp_t = carve([128, 512], F32)
            lm_t = carve([128, 512], F32)
            rs_t = carve([128, 512], F32)
            t1_t = carve([128, 512], F32)
            wb_t = [carve([128, 512], BF16) for _ in range(2)]
            wbr = Ring([(wb_t[i], Buf("sbw%d" % i), None) for i in range(2)])
            yo_s = [carve([128, 512], BF16) for _ in range(2)]
            yosr = Ring([(yo_s[i], Buf("sbyo%d" % i), "sbyo%d" % i) for i in range(2)])
            b_sp, b_lm, b_rs, b_t1 = Buf("sb_sp"), Buf("sb_lm"), Buf("sb_rs"), Buf("sb_t1")
            for h in range(8):
                P.op("sp", lambda e, h=h: e.dma_start(out=qT_s, in_=featT[(26 + h) * 128:(27 + h) * 128, :]),
                     reads=[b_feat], writes=[b_qs], dma=True, lane="sbq")
                P.op("sp", lambda e, h=h: e.dma_start(out=kT_s, in_=featT[(34 + h) * 128:(35 + h) * 128, :]),
                     reads=[b_feat], writes=[b_ks], dma=True, lane="sbk")
                P.op("sp", lambda e, h=h: e.dma_start(out=v_s, in_=tokM_v[:, :, 2304 + h * 128:2304 + (h + 1) * 128]),
                     reads=[b_tok], writes=[b_vs], dma=True, lane="sbv")
                for qg in range(S // 512):
                    q0 = qg * 512
                    klist = list(range(4 * qg + 3, -1, -1))
                    for ki, kc in enumerate(klist):
                        mi = kc - 4 * qg if kc >= 4 * qg else 4
                        P.op("pe", lambda e, kc=kc, q0=q0: e.matmul(
                            pbank[0][:, 0:512], lhsT=kT_s[:, kc * 128:(kc + 1) * 128], rhs=qT_s[:, q0:q0 + 512],
                            start=True, stop=True), reads=[b_ks, b_qs], writes=[b_pb[0]])
                        P.op("act", lambda e: e.activation(out=sp_t, in_=pbank[0][:, 0:512], func=AF.Exp, scale=SCALE),
                             reads=[b_pb[0]], writes=[b_sp])
                        P.op("act", lambda e: e.activation(out=sp_t, in_=sp_t, func=AF.Ln, bias=1.0),
                             reads=[b_sp], writes=[b_sp])
                        P.op("dve", lambda e, mi=mi: e.scalar_tensor_tensor(
                            out=lm_t, in0=sp_t, scalar=-1.0, in1=msb[:, mi, :], op0=ALU.mult, op1=ALU.mult),
                            reads=[b_sp, b_sc], writes=[b_lm])
                        P.op("pe", lambda e, ki=ki: e.matmul(pbank[1][:, 0:512], lhsT=ustr, rhs=lm_t,
                                                             start=True, stop=(ki == 0)),
                             reads=[b_lm, b_sc], writes=[b_pb[1]])
                        if ki > 0:
                            P.op("pe", lambda e: e.matmul(pbank[1][:, 0:512], lhsT=onesf, rhs=rs_t,
                                                          start=False, stop=True),
                                 reads=[b_rs, b_sc], writes=[b_pb[1]])
                        P.op("dve", lambda e: e.scalar_tensor_tensor(
                            out=t1_t, in0=pbank[0][:, 0:512], scalar=SCALE, in1=sp_t, op0=ALU.mult, op1=ALU.subtract),
                            reads=[b_pb[0], b_sp], writes=[b_t1])
                        P.op("dve", lambda e: e.tensor_tensor(out=t1_t, in0=t1_t, in1=pbank[1][:, 0:512], op=ALU.add),
                             reads=[b_t1, b_pb[1]], writes=[b_t1])
                        P.op("act", lambda e: e.activation(out=t1_t, in_=t1_t, func=AF.Exp),
                             reads=[b_t1], writes=[b_t1])
                        wbt, wbb, _ = wbr.next()
                        P.op("dve", lambda e, wbt=wbt, mi=mi: e.tensor_tensor(out=wbt, in0=t1_t, in1=msb[:, mi, :], op=ALU.mult),
                             reads=[b_t1, b_sc], writes=[wbb])
                        P.op("pe", lambda e, wbt=wbt, kc=kc, ki=ki, nk=len(klist): e.matmul(
                            pbank[2][:, 0:512], lhsT=v_s[:, kc, :], rhs=wbt, start=(ki == 0), stop=(ki == nk - 1)),
                            reads=[wbb, b_vs], writes=[b_pb[2]])
                        if ki == 0:
                            P.op("dve", lambda e: e.tensor_copy(out=rs_t, in_=lm_t), reads=[b_lm], writes=[b_rs])
                        else:
                            P.op("dve", lambda e: e.tensor_tensor(out=rs_t, in0=rs_t, in1=lm_t, op=ALU.add),
                                 reads=[b_lm, b_rs], writes=[b_rs])
                    yo, yob, yol = yosr.next()
                    P.op("act", lambda e, yo=yo: e.activation(out=yo, in_=pbank[2][:, 0:512], func=AF.Copy),
                         reads=[b_pb[2]], writes=[yob])
                    P.op("sp", lambda e, yo=yo, h=h, q0=q0: e.dma_start(
                        out=yT[(24 + h) * 128:(25 + h) * 128, q0:q0 + 512], in_=yo),
                        reads=[yob], writes=[b_yT], dma=True, lane=yol)

        if "moba" in MIXERS:
            P.barrier()
            carve_reset()
            NB = S // 256
            e8_t = carve([8, 1024], BF16)
            pmask = carve([128, 64], F32)
            mown_t = carve([128, 2, 256], BF16)
            b_mc = Buf("moba_consts")
            P.op("sp", lambda e: e.dma_start(out=e8_t, in_=c_e8), writes=[b_mc], dma=True, lane="mbc")
            P.op("sp", lambda e: e.dma_start(out=pmask, in_=c_pastmask), writes=[b_mc], dma=True, lane="mbc")
            P.op("sp", lambda e: e.dma_start(out=mown_t, in_=c_mown.rearrange("p (m t) -> p m t", m=2)),
                 writes=[b_mc], dma=True, lane="mbc")
            qT_m = carve([128, S], BF16)
            kT_m = carve([128, S], BF16)
            v_m = carve([128, NQB, 128], BF16)
            b_qm, b_km, b_vm = Buf("mb_q"), Buf("mb_k"), Buf("mb_v")
            kbar_f = carve([128, 8], F32)
            kbar_b = carve([128, 8], BF16)
            b_kb = Buf("kbar")
            gm_t = carve([128, 8], F32)
            top_t = carve([128, 8], F32)
            bias_f = carve([128, 8], F32)
            b_gm = Buf("gm")
            biasT = carve([8, S], BF16)
            b_bT = Buf("biasT")
            pm_t = [carve([128, 256], BF16) for _ in range(3)]
            pmr = Ring([(pm_t[i], Buf("mbp%d" % i), None) for i in range(3)])
            den_m = carve([128, 256], F32)
            b_dm = Buf("mb_den")
            yo_m = [carve([128, 256], BF16) for _ in range(2)]
            yomr = Ring([(yo_m[i], Buf("mbyo%d" % i), "mbyo%d" % i) for i in range(2)])
            for h in range(8):
                P.op("sp", lambda e, h=h: e.dma_start(out=qT_m, in_=featT[(10 + h) * 128:(11 + h) * 128, :]),
                     reads=[b_feat], writes=[b_qm], dma=True, lane="mbq")
                P.op("sp", lambda e, h=h: e.dma_start(out=kT_m, in_=featT[(18 + h) * 128:(19 + h) * 128, :]),
                     reads=[b_feat], writes=[b_km], dma=True, lane="mbk")
                P.op("sp", lambda e, h=h: e.dma_start(out=v_m, in_=tokM_v[:, :, 1280 + h * 128:1280 + (h + 1) * 128]),
                     reads=[b_tok], writes=[b_vm], dma=True, lane="mbv")
                P.op("dve", lambda e: e.memset(kbar_f, 0.0), writes=[b_kb])
                P.op("dve", lambda e: e.tensor_reduce(out=kbar_f[:, 0:NB], in_=kT_m.rearrange("p (n t) -> p n t", t=256),
                                                      axis=AX.X, op=ALU.add), reads=[b_km], writes=[b_kb])
                P.op("dve", lambda e: e.tensor_copy(out=kbar_b, in_=kbar_f), reads=[b_kb], writes=[b_kb])
                for qc in range(NQB):
                    own = qc // 2
                    P.op("pe", lambda e, qc=qc: e.matmul(pbank[0][:, 0:8], lhsT=qT_m[:, qc * 128:(qc + 1) * 128],
                                                         rhs=kbar_b, start=True, stop=True),
                         reads=[b_qm, b_kb], writes=[b_pb[0]])
                    P.op("dve", lambda e, own=own: e.tensor_tensor(out=gm_t, in0=pbank[0][:, 0:8],
                                                                   in1=pmask[:, own * 8:(own + 1) * 8], op=ALU.add),
                         reads=[b_pb[0], b_mc], writes=[b_gm])
                    P.op("dve", lambda e: e.max(out=top_t, in_=gm_t), reads=[b_gm], writes=[b_gm])
                    P.op("dve", lambda e: e.tensor_scalar(out=bias_f, in0=gm_t, scalar1=top_t[:, 2:3], scalar2=NEG,
                                                          op0=ALU.is_lt, op1=ALU.mult), reads=[b_gm], writes=[b_gm])
                    P.op("pe", lambda e: e.transpose(out=pbank[1][0:8, 0:128], in_=bias_f, identity=ident_f[:]),
                         reads=[b_gm, b_const], writes=[b_pb[1]])
                    P.op("act", lambda e, qc=qc: e.activation(out=biasT[:, qc * 128:(qc + 1) * 128],
                                                              in_=pbank[1][0:8, 0:128], func=AF.Copy),
                         reads=[b_pb[1]], writes=[b_bT])
                for b in range(NB):
                    q0 = b * 256
                    nk = 2 * b + 2
                    for kc in range(nk):
                        nb_ = kc // 2
                        P.op("pe", lambda e, kc=kc, q0=q0: e.matmul(
                            pbank[2][:, 0:256], lhsT=kT_m[:, kc * 128:(kc + 1) * 128], rhs=qT_m[:, q0:q0 + 256],
                            start=True, stop=False), reads=[b_km, b_qm], writes=[b_pb[2]])
                        if nb_ < b:
                            P.op("pe", lambda e, nb_=nb_, q0=q0: e.matmul(
                                pbank[2][:, 0:256], lhsT=e8_t[:, nb_ * 128:(nb_ + 1) * 128], rhs=biasT[:, q0:q0 + 256],
                                start=False, stop=True), reads=[b_mc, b_bT], writes=[b_pb[2]])
                        else:
                            P.op("pe", lambda e, m_=kc % 2: e.matmul(
                                pbank[2][:, 0:256], lhsT=ident_b[:], rhs=mown_t[:, m_, :],
                                start=False, stop=True), reads=[b_mc, b_const], writes=[b_pb[2]])
                        pm, pmb, _ = pmr.next()
                        P.op("act", lambda e, pm=pm: e.activation(out=pm, in_=pbank[2][:, 0:256], func=AF.Exp, scale=SCALE),
                             reads=[b_pb[2]], writes=[pmb])
                        P.op("pe", lambda e, pm=pm, kc=kc, nk=nk: e.matmul(
                            pbank[3][:, 0:256], lhsT=v_m[:, kc, :], rhs=pm, start=(kc == 0), stop=(kc == nk - 1)),
                            reads=[pmb, b_vm], writes=[b_pb[3]])
                        P.op("pe", lambda e, pm=pm, kc=kc, nk=nk: e.matmul(
                            pbank[1][:, 0:256], lhsT=ones_b[:], rhs=pm, start=(kc == 0), stop=(kc == nk - 1)),
                            reads=[pmb, b_const], writes=[b_pb[1]])
                    P.op("dve", lambda e: e.reciprocal(out=den_m, in_=pbank[1][:, 0:256]), reads=[b_pb[1]], writes=[b_dm])
                    yo, yob, yol = yomr.next()
                    P.op("dve", lambda e, yo=yo: e.tensor_tensor(out=yo, in0=pbank[3][:, 0:256], in1=den_m, op=ALU.mult),
                         reads=[b_pb[3], b_dm], writes=[yob])
                    P.op("sp", lambda e, yo=yo, h=h, q0=q0: e.dma_start(
                        out=yT[(16 + h) * 128:(17 + h) * 128, q0:q0 + 256], in_=yo),
                        reads=[yob], writes=[b_yT], dma=True, lane=yol)

        if "gla" in MIXERS:
            P.barrier()
            carve_reset()
            tn_t = carve([128, 128], F32)
            un_t = carve([128, 128], F32)
            mtri_t = carve([128, 128], F32)
            ones1 = carve([128, 128], F32)
            wup_t = carve([16, 512], F32)
            bup_t = carve([1, 512], F32)
            ng_t = carve([128, 2], F32)
            lr_t = carve([16, S], F32)
            b_gc = Buf("gla_consts")
            for dst_, src_, kw in ((tn_t, c_tn, {}), (un_t, c_un, {}), (mtri_t, c_mtri, {}), (ones1, c_ones_f, {}),
                                   (wup_t, gla_wg[l * 16:(l + 1) * 16, :], {}), (bup_t, gla_bg[l:l + 1, :], {}),
                                   (ng_t, gla_ng[l:l + 1, :].rearrange("o (c p) -> p (o c)", p=128),
                                    dict(allow_slow_non_contiguous=True)),
                                   (lr_t, glrT, {})):
                P.op("sp", lambda e, dst_=dst_, src_=src_, kw=kw: e.dma_start(out=dst_, in_=src_, **kw),
                     reads=[b_glr], writes=[b_gc], dma=True, lane="glc")
            SQ = 128 ** -0.5
            q_g = carve([128, S], F32)
            k_g = carve([128, S], F32)
            kt_g = carve([128, NQB, 128], F32)
            v_g = carve([128, NQB, 256], BF16)
            o_g = carve([128, 2, S], F32)
            b_qg, b_kg, b_ktg, b_vg, b_og = Buf("g_q"), Buf("g_k"), Buf("g_kt"), Buf("g_v"), Buf("g_o")
            L_t = carve([128, 128], F32)
            eq_t = carve([128, 128], F32)
            ek_t = carve([128, 128], F32)
            er_t = carve([128, 128], F32)
            qt_b = carve([128, 128], BF16)
            kt_b = carve([128, 128], BF16)
            kh_b = carve([128, 128], BF16)
            at_b = carve([128, 128], BF16)
            S_f = carve([128, 256], F32)
            S_b = carve([128, 256], BF16)
            b_L, b_eq, b_ek, b_er = Buf("g_L"), Buf("g_eq"), Buf("g_ek"), Buf("g_er")
            b_qt, b_ktb, b_kh, b_at, b_S = Buf("g_qt"), Buf("g_ktb"), Buf("g_kh"), Buf("g_at"), Buf("g_S")
            gg_t = carve([128, 512], F32)
            b_gg = Buf("g_gate")
            rs_g = carve([128, 512], F32)
            b_rsg = Buf("g_rstd")
            sq_b = carve([128, 2, 512], BF16)
            b_sq = Buf("g_sq")
            yo_g = [carve([128, 512], BF16) for _ in range(2)]
            yogr = Ring([(yo_g[i], Buf("gyo%d" % i), "gyo%d" % i) for i in range(2)])
            gktok_v = gktok.rearrange("(c p) f -> p c f", p=128)
            for hd in range(4):
                P.op("sp", lambda e, hd=hd: e.dma_start(out=q_g, in_=gfT[hd * 128:(hd + 1) * 128, :]),
                     reads=[b_gf], writes=[b_qg], dma=True, lane="glq")
                P.op("sp", lambda e, hd=hd: e.dma_start(out=k_g, in_=gfT[(4 + hd) * 128:(5 + hd) * 128, :]),
                     reads=[b_gf], writes=[b_kg], dma=True, lane="glk")
                P.op("sp", lambda e, hd=hd: e.dma_start(out=kt_g, in_=gktok_v[:, :, hd * 128:(hd + 1) * 128]),
                     reads=[b_gkt], writes=[b_ktg], dma=True, lane="glkt")
                P.op("sp", lambda e, hd=hd: e.dma_start(out=v_g, in_=tokM_v[:, :, hd * 256:(hd + 1) * 256]),
                     reads=[b_tok], writes=[b_vg], dma=True, lane="glv")
                for c in range(NQB):
                    cs = slice(c * 128, (c + 1) * 128)
                    P.op("pe", lambda e, cs=cs, hd=hd: e.matmul(pbank[0][:, 0:128], lhsT=lr_t[:, cs],
                                                                rhs=wup_t[:, hd * 128:(hd + 1) * 128], start=True, stop=False),
                         reads=[b_gc], writes=[b_pb[0]])
                    P.op("pe", lambda e, hd=hd: e.matmul(pbank[0][:, 0:128], lhsT=ones1[0:1, :],
                                                         rhs=bup_t[0:1, hd * 128:(hd + 1) * 128], start=False, stop=True),
                         reads=[b_gc], writes=[b_pb[0]])
                    P.op("act", lambda e: e.activation(out=L_t, in_=pbank[0][:, 0:128], func=AF.Exp, scale=-1.0),
                         reads=[b_pb[0]], writes=[b_L])
                    P.op("act", lambda e: e.activation(out=L_t, in_=L_t, func=AF.Ln, bias=1.0), reads=[b_L], writes=[b_L])
                    P.op("pe", lambda e: e.matmul(pbank[1][:, 0:128], lhsT=L_t, rhs=tn_t, start=True, stop=True),
                         reads=[b_L, b_gc], writes=[b_pb[1]])
                    P.op("pe", lambda e: e.matmul(pbank[2][:, 0:128], lhsT=un_t, rhs=L_t, start=True, stop=True),
                         reads=[b_L, b_gc], writes=[b_pb[2]])
                    P.op("act", lambda e: e.activation(out=eq_t, in_=pbank[1][:, 0:128], func=AF.Exp),
                         reads=[b_pb[1]], writes=[b_eq])
                    P.op("act", lambda e: e.activation(out=ek_t, in_=pbank[1][:, 0:128], func=AF.Exp, scale=-1.0),
                         reads=[b_pb[1]], writes=[b_ek])
                    P.op("act", lambda e: e.activation(out=er_t, in_=pbank[2][:, 0:128], func=AF.Exp),
                         reads=[b_pb[2]], writes=[b_er])
                    P.op("dve", lambda e, cs=cs: e.scalar_tensor_tensor(out=qt_b, in0=q_g[:, cs], scalar=SQ, in1=eq_t,
                                                                        op0=ALU.mult, op1=ALU.mult),
                         reads=[b_qg, b_eq], writes=[b_qt])
                    P.op("dve", lambda e, cs=cs: e.tensor_tensor(out=kt_b, in0=k_g[:, cs], in1=ek_t, op=ALU.mult),
                         reads=[b_kg, b_ek], writes=[b_ktb])
                    P.op("dve", lambda e, c=c: e.tensor_tensor(out=kh_b, in0=kt_g[:, c, :], in1=er_t, op=ALU.mult),
                         reads=[b_ktg, b_er], writes=[b_kh])
                    P.op("pe", lambda e: e.matmul(pbank[3][:, 0:128], lhsT=kt_b, rhs=qt_b, start=True, stop=True),
                         reads=[b_ktb, b_qt], writes=[b_pb[3]])
                    P.op("dve", lambda e: e.tensor_tensor(out=at_b, in0=pbank[3][:, 0:128], in1=mtri_t, op=ALU.mult),
                         reads=[b_pb[3], b_gc], writes=[b_at])
                    for ec in range(2):
                        pbo = 4 + ec
                        P.op("pe", lambda e, c=c, ec=ec, pbo=pbo: e.matmul(
                            pbank[pbo][:, 0:128], lhsT=v_g[:, c, ec * 128:(ec + 1) * 128], rhs=at_b,
                            start=True, stop=(c == 0)), reads=[b_vg, b_at], writes=[b_pb[pbo]])
                        if c > 0:
                            P.op("pe", lambda e, ec=ec, pbo=pbo: e.matmul(
                                pbank[pbo][:, 0:128], lhsT=S_b[:, ec * 128:(ec + 1) * 128], rhs=qt_b,
                                start=False, stop=True), reads=[b_S, b_qt], writes=[b_pb[pbo]])
                        P.op("act", lambda e, ec=ec, cs=cs, pbo=pbo: e.activation(out=o_g[:, ec, cs], in_=pbank[pbo][:, 0:128],
                                                                                  func=AF.Copy),
                             reads=[b_pb[pbo]], writes=[b_og])
                    if c < NQB - 1:
                        P.op("pe", lambda e, c=c: e.matmul(pbank[6][:, 0:256], lhsT=kh_b, rhs=v_g[:, c, :],
                                                           start=True, stop=True),
                             reads=[b_kh, b_vg], writes=[b_pb[6]])
                        if c == 0:
                            P.op("dve", lambda e: e.tensor_copy(out=S_f, in_=pbank[6][:, 0:256]),
                                 reads=[b_pb[6]], writes=[b_S])
                        else:
                            P.op("dve", lambda e: e.scalar_tensor_tensor(
                                out=S_f, in0=S_f, scalar=eq_t[:, 127:128], in1=pbank[6][:, 0:256],
                                op0=ALU.mult, op1=ALU.add), reads=[b_S, b_eq, b_pb[6]], writes=[b_S])
                        P.op("dve", lambda e: e.tensor_copy(out=S_b, in_=S_f), reads=[b_S], writes=[b_S])
                for qg in range(S // 512):
                    gs = slice(qg * 512, (qg + 1) * 512)
                    for ec in range(2):
                        P.op("act", lambda e, ec=ec, gs=gs: e.activation(out=sq_b[:, ec, :], in_=o_g[:, ec, gs], func=AF.Square),
                             reads=[b_og], writes=[b_sq])
                    for ec in range(2):
                        P.op("pe", lambda e, ec=ec: e.matmul(pbank[7][:, 0:512], lhsT=ones_b[:], rhs=sq_b[:, ec, :],
                                                             start=(ec == 0), stop=(ec == 1)),
                             reads=[b_sq, b_const], writes=[b_pb[7]])
                    P.op("dve", lambda e: e.tensor_scalar(out=rs_g, in0=pbank[7][:, 0:512], scalar1=1.0 / 256,
                                                          scalar2=RMS_EPS, op0=ALU.mult, op1=ALU.add),
                         reads=[b_pb[7]], writes=[b_rsg])
                    P.op("act", lambda e: e.activation(out=rs_g, in_=rs_g, func=AF.Sqrt), reads=[b_rsg], writes=[b_rsg])
                    P.op("dve", lambda e: e.reciprocal(out=rs_g, in_=rs_g), reads=[b_rsg], writes=[b_rsg])
                    for ec in range(2):
                        P.op("sp", lambda e, hd=hd, ec=ec, gs=gs: e.dma_start(
                            out=gg_t, in_=gfT[(8 + hd * 2 + ec) * 128:(9 + hd * 2 + ec) * 128, gs]),
                            reads=[b_gf], writes=[b_gg], dma=True, lane="glg")
                        P.op("act", lambda e: e.activation(out=gg_t, in_=gg_t, func=AF.Silu), reads=[b_gg], writes=[b_gg])
                        P.op("dve", lambda e, ec=ec: e.scalar_tensor_tensor(
                            out=gg_t, in0=gg_t, scalar=ng_t[:, ec:ec + 1], in1=rs_g, op0=ALU.mult, op1=ALU.mult),
                            reads=[b_gg, b_rsg, b_gc], writes=[b_gg])
                        yo, yob, yol = yogr.next()
                        P.op("dve", lambda e, yo=yo, ec=ec, gs=gs: e.tensor_tensor(out=yo, in0=o_g[:, ec, gs], in1=gg_t, op=ALU.mult),
                             reads=[b_og, b_gg], writes=[yob])
                        P.op("sp", lambda e, yo=yo, hd=hd, ec=ec, gs=gs: e.dma_start(
                            out=yT[(hd * 2 + ec) * 128:(hd * 2 + ec + 1) * 128, gs], in_=yo),
                            reads=[yob], writes=[b_yT], dma=True, lane=yol)

        P.barrier()
        xT_bufs = [b_xT] * KC
        for tt in range(NT):
            t0 = tt * T
            P.op("sp", lambda e, t0=t0: e.dma_start(out=xT[:], in_=yT_v[:, :, t0:t0 + T]),
                 reads=[b_yT], writes=[b_xT], dma=True, lane="yld")
            down_like(xT, xT_bufs, KC, wout_v, zo, tt, 1.0,
                      pcsrc=None if pc is None else pc["wo_b"], pcreads=pcreads)
        P.barrier()

    def final_stage(zi, ln):
        fin_ops = []
        orow = xr_t[0]
        zv = zs[zi].rearrange("(k p) s -> p k s", p=128)
        for tcn in range(S // 128):
            t0 = tcn * 128
            tt = t0 // T
            P.op("sp", lambda e, t0=t0: e.dma_start(out=tr_t, in_=zv[:, :, t0:t0 + 128]),
                 reads=[zbuf[zi][tt]], writes=[b_tr], dma=True, lane="trs")
            for g in range(KC // 4):
                bk = g % 2
                for j in range(4):
                    kc = g * 4 + j
                    P.op("dve", lambda e, kc=kc, t0=t0: e.tensor_tensor(
                        out=tr_t[:, kc, :], in0=tr_t[:, kc, :], in1=mean_bc[:, t0:t0 + 128], op=ALU.subtract),
                        reads=[b_tr, b_stats[tt]], writes=[b_tr])
                    P.op("dve", lambda e, kc=kc, t0=t0: e.tensor_tensor(
                        out=tr_t[:, kc, :], in0=tr_t[:, kc, :], in1=rstd_bc[:, t0:t0 + 128], op=ALU.mult),
                        reads=[b_tr, b_stats[tt]], writes=[b_tr])
                    P.op("act", lambda e, kc=kc: e.activation(
                        out=tr_t[:, kc, :], in_=tr_t[:, kc, :], func=AF.Identity,
                        scale=lng[:, ln, kc:kc + 1], bias=lnb[:, ln, kc:kc + 1]),
                        reads=[b_tr, b_const], writes=[b_tr])
                    P.op("pe", lambda e, kc=kc, j=j, bk=bk: e.transpose(
                        out=pbank[bk][:, j * 128:(j + 1) * 128], in_=tr_t[:, kc, :], identity=ident_f[:]),
                        reads=[b_tr, b_const], writes=[b_pb[bk]])
                if g % 2:
                    P.op("act", lambda e, g=g, bk=bk: e.activation(
                        out=orow[:, g * 512:(g + 1) * 512], in_=pbank[bk][:], func=AF.Copy),
                        reads=[b_pb[bk]], writes=[b_xr])
                else:
                    P.op("dve", lambda e, g=g, bk=bk: e.tensor_copy(
                        out=orow[:, g * 512:(g + 1) * 512], in_=pbank[bk][:]),
                        reads=[b_pb[bk]], writes=[b_xr])
            fo = P.op("sp", lambda e, t0=t0: e.dma_start(out=out[t0:t0 + 128, :], in_=orow),
                      reads=[b_xr], writes=[Buf("out")], dma=True, lane="outst")
            fin_ops.append(fo)
        return fin_ops

    zi = 0
    ln = None
    plan = []
    for l in range(L):
        if "ffn1" in phases:
            plan.append(("ffn", l, 0))
        if "mix" in phases:
            plan.append(("mix", l, 0))
        if "ffn2" in phases:
            plan.append(("ffn", l, 1))
    pcs = [None] * len(plan)
    for i, (kind, l, which) in enumerate(plan):
        if PRECAST and i + 1 < len(plan):
            nk, nl, nw = plan[i + 1]
            pcs[i + 1] = mk_precast_ffn(nl, nw) if nk == "ffn" else mk_precast_mix(nl)
            pacer.load(pcs[i + 1], FFN_STEPS if kind == "ffn" else MIX_STEPS)
        if kind == "ffn":
            ffn(l, which, zi, 1 - zi, ln, pc=pcs[i])
            zi, ln = 1 - zi, l * 3 + (0 if which == 0 else 2)
        else:
            mixer(l, zi, 1 - zi, ln, pc=pcs[i])
            zi, ln = 1 - zi, l * 3 + 1
        pacer.flush()
    P.barrier()
    fin = final_stage(zi, ln)
    P.emit(nc, st, fin[-1:])
    st.close()
    return nc, P


def kernel(**inputs):
    cfg = Cfg()
    nc, _ = build(cfg)
    consts = host_consts()
    L = cfg.L
    x = np.ascontiguousarray(np.asarray(inputs["x"], dtype=np.float32))
    pos = np.ascontiguousarray(np.asarray(inputs["positions"], dtype=np.int32))
    n = x.shape[0]

    def flat(name):
        a = np.asarray(inputs[name], dtype=np.float32)
        return np.ascontiguousarray(a.reshape((a.shape[0] * a.shape[1],) + a.shape[2:]))

    shared = {k: flat(k) for k in ("w_in", "w_out", "ffn1_w_gu", "ffn1_w_down", "ffn2_w_gu", "ffn2_w_down",
                                   "ln_g", "ln_b", "gla_w_gate_up")}
    for k in ("gla_b_gate_up", "gla_norm_g", "swa_sinks"):
        shared[k] = np.ascontiguousarray(np.asarray(inputs[k], dtype=np.float32))
    in_maps = [dict(x=x[i], positions=pos[i:i + 1], **shared, **consts) for i in range(n)]
    res = run_bass_kernel_spmd(nc, in_maps, core_ids=list(range(n)))
    return np.stack([np.asarray(r["out"], dtype=np.float32) for r in res.results], axis=0)
```

```python
import math
from contextlib import ExitStack

import numpy as np
import ml_dtypes

import concourse.bass as bass
import concourse.mybir as mybir
from concourse.bass_utils import run_bass_kernel_spmd

F32 = mybir.dt.float32
BF16 = mybir.dt.bfloat16
I32 = mybir.dt.int32
AF = mybir.ActivationFunctionType
ALU = mybir.AluOpType
AX = mybir.AxisListType

D = 4096
KC = D // 128
D_IN = 10768
LN_EPS = 1e-5
RMS_EPS = 1e-5
NEG = -30000.0


class Buf:
    __slots__ = ("name", "last_w", "readers")

    def __init__(self, name):
        self.name = name
        self.last_w = None
        self.readers = []


class Op:
    __slots__ = ("eng", "fn", "deps", "dma", "lane", "sig", "val", "idx")

    def __init__(self, eng, fn, dma, lane):
        self.eng = eng
        self.fn = fn
        self.deps = set()
        self.dma = dma
        self.lane = lane
        self.sig = False
        self.val = None
        self.idx = None


ENGS = ("pe", "act", "dve", "pool", "sp")


class Prog:
    def __init__(self):
        self.ops = []
        self.per_eng = {e: [] for e in ENGS}
        self.barrier_deps = {e: set() for e in ENGS}
        self.last_dma = {}

    def op(self, eng, fn, reads=(), writes=(), dma=False, lane=None, extra=()):
        o = Op(eng, fn, dma, lane)
        o.idx = len(self.ops)
        for x_ in extra:
            if x_ is not None:
                o.deps.add(x_)
        if dma:
            assert lane is not None
        for b in reads:
            if b.last_w is not None:
                o.deps.add(b.last_w)
        for b in writes:
            if b.last_w is not None:
                o.deps.add(b.last_w)
            lastr = {}
            for r in b.readers:
                if r.dma:
                    o.deps.add(r)
                elif r.eng not in lastr or lastr[r.eng].idx < r.idx:
                    lastr[r.eng] = r
            for r in lastr.values():
                o.deps.add(r)
        for b in writes:
            b.last_w = o
            b.readers = []
        for b in reads:
            if b.last_w is not o:
                b.readers.append(o)
        if self.barrier_deps[eng]:
            o.deps |= self.barrier_deps[eng]
            self.barrier_deps[eng] = set()
        o.deps.discard(o)
        self.ops.append(o)
        self.per_eng[eng].append(o)
        if dma:
            self.last_dma[lane] = o
        return o

    def barrier(self):
        pend = set()
        for e in ENGS:
            for o in reversed(self.per_eng[e]):
                if not o.dma:
                    pend.add(o)
                    break
        for o in self.last_dma.values():
            pend.add(o)
        for e in ENGS:
            self.barrier_deps[e] |= pend

    def emit(self, nc, stack, final_ops):
        for o in self.ops:
            for d in o.deps:
                if d.dma:
                    d.sig = True
                elif d.eng == "pe" and o.eng == "pe" and not o.dma:
                    pass
                else:
                    d.sig = True
        for o in final_ops:
            o.sig = True
        eng_sem = {e: stack.enter_context(nc.semaphore("s_" + e)) for e in ENGS}
        lane_sem = {}
        cnt = {e: 0 for e in ENGS}
        lane_cnt = {}
        for o in self.ops:
            if not o.sig:
                continue
            if o.dma:
                if o.lane not in lane_sem:
                    lane_sem[o.lane] = stack.enter_context(
                        nc.semaphore("l_%d" % len(lane_sem)))
                    lane_cnt[o.lane] = 0
                lane_cnt[o.lane] += 16
                o.val = (lane_sem[o.lane], lane_cnt[o.lane])
            else:
                cnt[o.eng] += 1
                o.val = (eng_sem[o.eng], cnt[o.eng])
        self.stats = dict(n_ops=len(self.ops), sig=dict(cnt), lanes=len(lane_sem),
                          per_eng={e: len(v) for e, v in self.per_eng.items()})
        block = stack.enter_context(nc.Block())

        def run(eng_name, handle):
            seen = {}
            for o in self.per_eng[eng_name]:
                need = {}
                for d in o.deps:
                    if (not d.dma) and d.eng == "pe" and eng_name == "pe" and not o.dma:
                        continue
                    sem, v = d.val
                    k = id(sem)
                    if seen.get(k, 0) < v and need.get(k, (None, 0))[1] < v:
                        need[k] = (sem, v)
                for k, (sem, v) in need.items():
                    handle.wait_ge(sem, v)
                    seen[k] = v
                ins = o.fn(handle)
                if o.sig:
                    sem, v = o.val
                    ins.then_inc(sem, 16 if o.dma else 1)
            for o in final_ops:
                if eng_name == "sp":
                    sem, v = o.val
                    if seen.get(id(sem), 0) < v:
                        handle.wait_ge(sem, v)
                        seen[id(sem)] = v

        @block.tensor
        def _(h):
            run("pe", h)

        @block.scalar
        def _(h):
            run("act", h)

        @block.vector
        def _(h):
            run("dve", h)

        @block.gpsimd
        def _(h):
            run("pool", h)

        @block.sync
        def _(h):
            run("sp", h)


class Ring:
    def __init__(self, slots):
        self.slots = slots
        self.i = 0

    def next(self):
        s = self.slots[self.i % len(self.slots)]
        self.i += 1
        return s


class Cfg:
    def __init__(self, S=2048, DFF=11008, L=2, T=512):
        self.S, self.DFF, self.L, self.T = S, DFF, L, T
        self.NF = DFF // 128
        self.NT = S // T
        self.alpha = float((2 * L) ** 0.25)


def host_consts():
    c = {}
    c["ident_f"] = np.eye(128, dtype=np.float32)
    c["ones_b"] = np.ones((128, 128), dtype=ml_dtypes.bfloat16)
    c["ident_b"] = np.eye(128, dtype=ml_dtypes.bfloat16)
    perm = np.zeros((128, 128), np.float32)
    for m in range(128):
        perm[(m + 64) % 128, m] = 1.0
    c["perm_f"] = perm
    inv = 1.0 / (10000.0 ** (np.arange(0, 128, 2, dtype=np.float32) / np.float32(128)))
    c["inv2"] = np.concatenate([inv, inv]).astype(np.float32).reshape(128, 1)
    sgn = np.concatenate([-np.ones(64), np.ones(64)]).astype(np.float32).reshape(128, 1)
    c["sgn2"] = sgn
    tk = np.arange(128)[:, None]
    tq = np.arange(128)[None, :]
    cur = np.where(tk <= tq, 0.0, NEG).astype(np.float32)
    prv = np.where(tk > tq, 0.0, NEG).astype(np.float32)
    c["mask_cur4"] = np.tile(cur, (1, 4)).astype(ml_dtypes.bfloat16)
    c["mask_prev4"] = np.tile(prv, (1, 4)).astype(ml_dtypes.bfloat16)
    i = np.arange(128)[:, None]
    j = np.arange(512)[None, :]
    msb = [(j > m * 128 + i).astype(np.float32) for m in range(4)] + [np.ones((128, 512), np.float32)]
    c["msb5"] = np.concatenate(msb, axis=1)
    a = np.arange(128)
    c["ustrict_f"] = (a[:, None] > a[None, :]).astype(np.float32)
    c["ones_f"] = np.ones((128, 128), np.float32)
    e8 = np.zeros((8, 8 * 128), np.float32)
    for n in range(8):
        e8[n, n * 128:(n + 1) * 128] = 1.0
    c["e8"] = e8.astype(ml_dtypes.bfloat16)
    pm = np.zeros((128, 64), np.float32)
    for own in range(8):
        for n in range(8):
            pm[:, own * 8 + n] = 0.0 if n < own else -1e30
    c["pastmask"] = pm
    jq = np.arange(256)[None, :]
    mown = [np.where(i + m * 128 <= jq, 0.0, NEG).astype(np.float32) for m in range(2)]
    c["mown"] = np.concatenate(mown, axis=1).astype(ml_dtypes.bfloat16)
    c["tn_f"] = np.where(a[:, None] <= a[None, :], -1.0 / 16, 0.0).astype(np.float32)
    c["un_f"] = np.where(a[:, None] > a[None, :], -1.0 / 16, 0.0).astype(np.float32)
    c["mtri01"] = (a[:, None] <= a[None, :]).astype(np.float32)
    return c


def build(cfg, phases=("ffn1", "mix", "ffn2"), ZERO_Y_ROWS=(), MIXERS=("gla", "swa", "moba", "sb"), PRECAST=True):
    S, T, NF, NT, L = cfg.S, cfg.T, cfg.NF, cfg.NT, cfg.L
    alpha = cfg.alpha
    nc = bass.Bass("TRN2", target_bir_lowering=False)
    P = Prog()
    st = ExitStack()

    def din(name, shape, dt=F32):
        return nc.dram_tensor(name, list(shape), dt, kind="ExternalInput").ap()

    def dscr(name, shape, dt=F32):
        return nc.dram_tensor(name, list(shape), dt, kind="Internal").ap()

    x_in = din("x", [S, D])
    w_gu = [din("ffn1_w_gu", [L * D, 2 * cfg.DFF]), din("ffn2_w_gu", [L * D, 2 * cfg.DFF])]
    w_dn = [din("ffn1_w_down", [L * cfg.DFF, D]), din("ffn2_w_down", [L * cfg.DFF, D])]
    ln_g = din("ln_g", [L * 3, D])
    ln_b = din("ln_b", [L * 3, D])
    pos_in = nc.dram_tensor("positions", [1, S], I32, kind="ExternalInput").ap()
    w_in = din("w_in", [L * D, D_IN])
    w_out = din("w_out", [L * D, D])
    gla_wg = din("gla_w_gate_up", [L * 16, 512])
    gla_bg = din("gla_b_gate_up", [L, 512])
    gla_ng = din("gla_norm_g", [L, 256])
    sinks_in = din("swa_sinks", [L, 8])
    c_ident_b = din("ident_b", [128, 128], BF16)
    c_perm_f = din("perm_f", [128, 128])
    c_inv2 = din("inv2", [128, 1])
    c_sgn2 = din("sgn2", [128, 1])
    c_mcur = din("mask_cur4", [128, 512], BF16)
    c_mprev = din("mask_prev4", [128, 512], BF16)
    featT = dscr("featT", [42 * 128, S], BF16)
    tokM = dscr("tokM", [S, 3328], BF16)
    yT = dscr("yT", [D, S], BF16)
    b_feat = Buf("featT")
    b_tok = Buf("tokM")
    b_yT = Buf("yT")
    c_msb5 = din("msb5", [128, 5 * 512])
    c_ustrict = din("ustrict_f", [128, 128])
    c_ones_f = din("ones_f", [128, 128])
    c_e8 = din("e8", [8, 1024], BF16)
    c_pastmask = din("pastmask", [128, 64])
    c_mown = din("mown", [128, 512], BF16)
    c_tn = din("tn_f", [128, 128])
    c_un = din("un_f", [128, 128])
    c_mtri = din("mtri01", [128, 128])
    gfT = dscr("gfT", [16 * 128, S])
    glrT = dscr("glrT", [16, S])
    gktok = dscr("gktok", [S, 512])
    b_gf = Buf("gfT")
    b_glr = Buf("glrT")
    b_gkt = Buf("gktok")
    c_ident_f = din("ident_f", [128, 128])
    c_ones_b = din("ones_b", [128, 128], BF16)
    out = nc.dram_tensor("out", [S, D], F32, kind="ExternalOutput").ap()

    zs = [dscr("zA", [D, S]), dscr("zB", [D, S])]
    xaT = dscr("xaT", [D, S])
    zbuf = [[Buf("z%d_%d" % (i, t)) for t in range(NT)] for i in range(2)]
    xabuf = [Buf("xa_%d" % t) for t in range(NT)]

    def sb(name, shape, dt):
        return st.enter_context(nc.sbuf_tensor(name, list(shape), dt))

    def ps(name, shape, dt=F32):
        return st.enter_context(nc.psum_tensor(name, list(shape), dt))

    ident_f = sb("ident_f_sb", [128, 128], F32)
    ones_b = sb("ones_b_sb", [128, 128], BF16)
    lng = sb("lng", [128, L * 3, KC], F32)
    lnb = sb("lnb", [128, L * 3, KC], F32)
    lnga = sb("lnga", [128, L * 3, KC], F32)
    lnba = sb("lnba", [128, L * 3, KC], F32)
    ident_b = sb("ident_b_sb", [128, 128], BF16)
    perm_f = sb("perm_f_sb", [128, 128], F32)
    inv2 = sb("inv2_sb", [128, 1], F32)
    sgn2 = sb("sgn2_sb", [128, 1], F32)
    mcur = sb("mcur_sb", [128, 512], BF16)
    mprev = sb("mprev_sb", [128, 512], BF16)
    mean_bc = sb("mean_bc", [128, S], F32)
    rstd_bc = sb("rstd_bc", [128, S], F32)
    b_const = Buf("consts")
    b_stats = [Buf("stats_%d" % t) for t in range(NT)]

    xT = sb("xT", [128, KC, T], BF16)
    b_xT = Buf("xT")
    ARENA = max(NF * T, 40 * 1024)
    arena = sb("arena", [128, ARENA], BF16)
    AT = arena[:, 0:NF * T].rearrange("p (f t) -> p f t", f=NF)
    b_AT = [Buf("AT%d" % f) for f in range(NF)]
    _aoff = [0]

    def carve_reset():
        _aoff[0] = 0

    def carve(shape, dt):
        n = 1
        for s_ in shape[1:]:
            n *= s_
        nb = n * (4 if dt == F32 else 2)
        nb = (nb + 63) // 64 * 64
        o = _aoff[0]
        _aoff[0] += nb // 2
        assert _aoff[0] <= ARENA, ("arena overflow", _aoff[0], ARENA)
        v = arena[:, o:o + nb // 2]
        if dt == F32:
            v = v.bitcast(F32)
        v = v[0:shape[0], 0:n]
        if len(shape) == 3:
            v = v.rearrange("p (a b) -> p a b", a=shape[1])
        elif len(shape) == 4:
            v = v.rearrange("p (a b c) -> p a b c", a=shape[1], b=shape[2])
        return v
    NRING = 3
    ring_t = [sb("wr%d" % i, [128, 16 * 512], BF16) for i in range(NRING)]
    ring = Ring([(ring_t[i], (Buf("wr%da" % i), Buf("wr%db" % i)), "wr%d" % i) for i in range(NRING)])
    ld_t = [sb("ld%d" % i, [128, T], F32) for i in range(2)]
    ldr = Ring([(ld_t[i], Buf("ld%d" % i), "ld%d" % i) for i in range(2)])
    st_t = [sb("stg%d" % i, [128, T], F32) for i in range(2)]
    stg = Ring([(st_t[i], Buf("stg%d" % i), "stg%d" % i) for i in range(2)])
    sg_t = [sb("sg%d" % i, [128, T], F32) for i in range(2)]
    sgr = Ring([(sg_t[i], Buf("sg%d" % i), "sg%d" % i) for i in range(2)])
    zb_t = [sb("zb%d" % i, [128, 2, T], BF16) for i in range(2)]
    zbr = Ring([(zb_t[i], Buf("zbq%d" % i), None) for i in range(2)])
    tmp_f = sb("tmp_f", [128, T], F32)
    b_tmp = Buf("tmp_f")

    pbank = [ps("pb%d" % i, [128, 512]) for i in range(8)]
    b_pb = [Buf("pb%d" % i) for i in range(8)]

    P.op("sp", lambda e: e.dma_start(out=ident_f[:], in_=c_ident_f), writes=[b_const], dma=True, lane="c0")
    P.op("sp", lambda e: e.dma_start(out=ones_b[:], in_=c_ones_b), writes=[b_const], dma=True, lane="c1")
    P.op("sp", lambda e: e.dma_start(out=lng[:], in_=ln_g.rearrange("j (k p) -> p j k", p=128),
                                     allow_slow_non_contiguous=True),
         writes=[b_const], dma=True, lane="c2")
    P.op("sp", lambda e: e.dma_start(out=lnb[:], in_=ln_b.rearrange("j (k p) -> p j k", p=128),
                                     allow_slow_non_contiguous=True),
         writes=[b_const], dma=True, lane="c3")
    for i_, (dst_, src_) in enumerate(((ident_b, c_ident_b), (perm_f, c_perm_f), (inv2, c_inv2), (sgn2, c_sgn2),
                                       (mcur, c_mcur), (mprev, c_mprev))):
        P.op("sp", lambda e, dst_=dst_, src_=src_: e.dma_start(out=dst_[:], in_=src_), writes=[b_const],
             dma=True, lane="c%d" % (4 + i_))
    P.op("dve", lambda e: e.tensor_scalar(out=lnga[:], in0=lng[:], scalar1=alpha, scalar2=None, op0=ALU.mult),
         reads=[b_const], writes=[b_const])
    P.op("dve", lambda e: e.tensor_scalar(out=lnba[:], in0=lnb[:], scalar1=alpha, scalar2=None, op0=ALU.mult),
         reads=[b_const], writes=[b_const])

    carve_reset()
    xr_t = [carve([128, D], F32)]
    b_xr = Buf("xr")
    tr_t = carve([128, KC, 128], F32)
    b_tr = Buf("trs")
    zA_v = zs[0].rearrange("(k p) s -> p k s", p=128)
    for tcn in range(S // 128):
        t0 = tcn * 128
        tt = t0 // T
        P.op("sp", lambda e, t0=t0: e.dma_start(out=xr_t[0], in_=x_in[t0:t0 + 128, :]),
             writes=[b_xr], dma=True, lane="xr")
        for g in range(KC // 4):
            bk = g % 2
            for j in range(4):
                kc = g * 4 + j
                P.op("pe", lambda e, kc=kc, j=j, bk=bk: e.transpose(
                    out=pbank[bk][:, j * 128:(j + 1) * 128], in_=xr_t[0][:, kc * 128:(kc + 1) * 128],
                    identity=ident_f[:]), reads=[b_xr, b_const], writes=[b_pb[bk]])
            P.op("act" if g % 2 else "dve",
                 (lambda e, g=g, bk=bk: e.activation(
                     out=tr_t[:, g * 4:(g + 1) * 4, :],
                     in_=pbank[bk][:].rearrange("p (j t) -> p j t", j=4), func=AF.Copy)) if g % 2 else
                 (lambda e, g=g, bk=bk: e.tensor_copy(
                     out=tr_t[:, g * 4:(g + 1) * 4, :],
                     in_=pbank[bk][:].rearrange("p (j t) -> p j t", j=4))),
                 reads=[b_pb[bk]], writes=[b_tr])
        P.op("sp", lambda e, t0=t0: e.dma_start(out=zA_v[:, :, t0:t0 + 128], in_=tr_t),
             reads=[b_tr], writes=[zbuf[0][tt]], dma=True, lane="trs")

    P.barrier()

    def load_stage(zi, tt, ln, kcs=None, deep=False):
        t0 = tt * T
        kcs = list(range(KC)) if kcs is None else list(kcs)
        slots = [ldr.slots[0], ldr.slots[1]] + ([sgr.slots[0], sgr.slots[1]] if deep else [])
        npre = len(slots) - 1
        loaded = {}

        def issue_load(i):
            kc = kcs[i]
            lt, lb, ll = slots[i % len(slots)]
            P.op(AQ[0], lambda e, kc=kc, lt=lt: e.dma_start(
                out=lt[:], in_=zs[zi][kc * 128:(kc + 1) * 128, t0:t0 + T]),
                reads=[zbuf[zi][tt]], writes=[lb], dma=True, lane=ll)
            loaded[i] = (lt, lb)

        for i in range(min(npre, len(kcs))):
            issue_load(i)
        for i, kc in enumerate(kcs):
            if i + npre < len(kcs):
                issue_load(i + npre)
            lt, lb = loaded.pop(i)
            sx, sxb, sl = stg.next()
            if ln is None:
                P.op("act", lambda e, kc=kc, lt=lt: e.activation(out=xT[:, kc, :], in_=lt[:], func=AF.Copy),
                     reads=[lb], writes=[b_xT])
                P.op("dve", lambda e, lt=lt, sx=sx: e.tensor_scalar(
                    out=sx[:], in0=lt[:], scalar1=alpha, scalar2=None, op0=ALU.mult),
                    reads=[lb], writes=[sxb])
            else:
                P.op("dve", lambda e, lt=lt: e.tensor_tensor(
                    out=lt[:], in0=lt[:], in1=mean_bc[:, t0:t0 + T], op=ALU.subtract),
                    reads=[lb, b_stats[tt]], writes=[lb])
                P.op("dve", lambda e, lt=lt: e.tensor_tensor(
                    out=lt[:], in0=lt[:], in1=rstd_bc[:, t0:t0 + T], op=ALU.mult),
                    reads=[lb, b_stats[tt]], writes=[lb])
                P.op("act", lambda e, kc=kc, lt=lt: e.activation(
                    out=xT[:, kc, :], in_=lt[:], func=AF.Identity,
                    scale=lng[:, ln, kc:kc + 1], bias=lnb[:, ln, kc:kc + 1]),
                    reads=[lb, b_const], writes=[b_xT])
                P.op("act", lambda e, kc=kc, lt=lt, sx=sx: e.activation(
                    out=sx[:], in_=lt[:], func=AF.Identity,
                    scale=lnga[:, ln, kc:kc + 1], bias=lnba[:, ln, kc:kc + 1]),
                    reads=[lb, b_const], writes=[sxb])
            P.op(AQ[0], lambda e, kc=kc, sx=sx: e.dma_start(
                out=xaT[kc * 128:(kc + 1) * 128, t0:t0 + T], in_=sx[:]),
                reads=[sxb], writes=[xabuf[tt]], dma=True, lane=sl)

    def epilogue_prefetch(tt, dc):
        t0 = tt * T
        lt, lb, ll = sgr.next()
        P.op(AQ[0], lambda e: e.dma_start(out=lt[:], in_=xaT[dc * 128:(dc + 1) * 128, t0:t0 + T]),
             reads=[xabuf[tt]], writes=[lb], dma=True, lane=ll)
        return lt, lb

    def epilogue(zo, tt, dc, pb_i, res_scale, pre):
        t0 = tt * T
        lt, lb = pre
        sx, sxb, sl = stg.next()
        P.op("dve", lambda e: e.scalar_tensor_tensor(
            out=sx[:], in0=pbank[pb_i][:, 0:T], scalar=res_scale, in1=lt[:], op0=ALU.mult, op1=ALU.add),
            reads=[b_pb[pb_i], lb], writes=[sxb])
        P.op(AQ[0], lambda e: e.dma_start(out=zs[zo][dc * 128:(dc + 1) * 128, t0:t0 + T], in_=sx[:]),
             reads=[sxb], writes=[zbuf[zo][tt]], dma=True, lane=sl)
        zq, zqb, _ = zbr.next()
        P.op("act", lambda e: e.activation(out=zq[:, 0, :], in_=sx[:], func=AF.Copy),
             reads=[sxb], writes=[zqb])
        P.op("act", lambda e: e.activation(out=zq[:, 1, :], in_=sx[:], func=AF.Square),
             reads=[sxb], writes=[zqb])
        P.op("pe", lambda e: e.matmul(pbank[6][:, 0:T], lhsT=ones_b[:], rhs=zq[:, 0, :],
                                      start=(dc == 0), stop=(dc == KC - 1)),
             reads=[zqb, b_const], writes=[b_pb[6]])
        P.op("pe", lambda e: e.matmul(pbank[7][:, 0:T], lhsT=ones_b[:], rhs=zq[:, 1, :],
                                      start=(dc == 0), stop=(dc == KC - 1)),
             reads=[zqb, b_const], writes=[b_pb[7]])
        if dc == KC - 1:
            P.op("act", lambda e: e.activation(out=mean_bc[:, t0:t0 + T], in_=pbank[6][:, 0:T],
                                               func=AF.Copy, scale=1.0 / D),
                 reads=[b_pb[6]], writes=[b_stats[tt]])
            P.op("dve", lambda e: e.tensor_tensor(out=tmp_f[:], in0=mean_bc[:, t0:t0 + T],
                                                  in1=mean_bc[:, t0:t0 + T], op=ALU.mult),
                 reads=[b_stats[tt]], writes=[b_tmp])
            P.op("dve", lambda e: e.scalar_tensor_tensor(
                out=tmp_f[:], in0=pbank[7][:, 0:T], scalar=1.0 / D, in1=tmp_f[:],
                op0=ALU.mult, op1=ALU.subtract), reads=[b_pb[7], b_tmp], writes=[b_tmp])
            P.op("dve", lambda e: e.tensor_scalar(out=tmp_f[:], in0=tmp_f[:], scalar1=LN_EPS, scalar2=None,
                                                  op0=ALU.add), reads=[b_tmp], writes=[b_tmp])
            P.op("act", lambda e: e.activation(out=tmp_f[:], in_=tmp_f[:], func=AF.Sqrt),
                 reads=[b_tmp], writes=[b_tmp])
            P.op("dve", lambda e: e.reciprocal(out=rstd_bc[:, t0:t0 + T], in_=tmp_f[:]),
                 reads=[b_tmp], writes=[b_stats[tt]])

    AQ = ["sp"]

    class Pacer:
        def __init__(self):
            self.q = []
            self.lane = None
            self.rate = 0.0
            self.acc = 0.0

        def load(self, pc, nsteps):
            self.flush()
            self.q = list(pc["ops"])
            self.lane = pc["lane"]
            self.rate = len(self.q) / max(1.0, 0.8 * nsteps)
            self.acc = 0.0

        def tick(self, after=None):
            self.acc += self.rate
            while self.acc >= 1.0 and self.q:
                self.acc -= 1.0
                fn, b = self.q.pop(0)
                P.op("pool", fn, writes=[b], dma=True, lane=self.lane, extra=[after])

        def flush(self):
            while self.q:
                fn, b = self.q.pop(0)
                P.op("pool", fn, writes=[b], dma=True, lane=self.lane)

    pacer = Pacer()
    DG = D // 256

    def mk_precast_ffn(l, which):
        gu_b = dscr("gub_%d_%d" % (l, which), [NF * 128, KC * 256], BF16)
        dn_b = dscr("dnb_%d_%d" % (l, which), [DG * 128, NF * 256], BF16)
        wg = w_gu[which][l * D:(l + 1) * D, :].rearrange("(k p) f -> p k f", p=128)
        wd = w_dn[which][l * cfg.DFF:(l + 1) * cfg.DFF, :].rearrange("(c p) d -> p c d", p=128)
        ops, bufs = [], []
        for fc in range(NF):
            for half in (0, 1):
                b = Buf("pc")
                bufs.append(b)
                ops.append((lambda e, fc=fc, half=half: e.dma_start(
                    out=gu_b[fc * 128:(fc + 1) * 128, :].rearrange("p (k c) -> p k c", k=KC)[:, :, half * 128:(half + 1) * 128],
                    in_=wg[:, :, half * cfg.DFF + fc * 128:half * cfg.DFF + (fc + 1) * 128]), b))
        for dg in range(DG):
            b = Buf("pc")
            bufs.append(b)
            ops.append((lambda e, dg=dg: e.dma_start(
                out=dn_b[dg * 128:(dg + 1) * 128, :].rearrange("p (c d) -> p c d", c=NF),
                in_=wd[:, :, dg * 256:(dg + 1) * 256]), b))
        return dict(gu_b=gu_b, dn_b=dn_b, ops=ops, bufs=bufs, lane="pcf%d%d" % (l, which))

    FFN_STEPS = NT * (NF + DG * ((NF + 31) // 32))

    def ffn(l, which, zi, zo, ln, pc=None):
        wg = w_gu[which][l * D:(l + 1) * D, :].rearrange("(k p) f -> p k f", p=128)
        wd = w_dn[which][l * cfg.DFF:(l + 1) * cfg.DFF, :].rearrange("(c p) d -> p c d", p=128)
        AQ[0] = "sp" if pc is None else "pool"
        first = [True]

        def pcreads():
            if pc is not None and first[0]:
                first[0] = False
                return list(pc["bufs"])
            return []

        load_stage(zi, 0, ln, deep=True)
        for tt in range(NT):
            for fc in range(NF):
                wt, wbs, wl = ring.next()
                wv = wt[:, 0:KC * 256].rearrange("p (k c) -> p k c", k=KC)
                if pc is None:
                    P.op("pool", lambda e, fc=fc, wv=wv: e.dma_start(
                        out=wv[:, :, 0:128], in_=wg[:, :, fc * 128:(fc + 1) * 128]),
                        writes=[wbs[0]], dma=True, lane=wl + "g")
                    lop = P.op("pool", lambda e, fc=fc, wv=wv: e.dma_start(
                        out=wv[:, :, 128:256], in_=wg[:, :, cfg.DFF + fc * 128:cfg.DFF + (fc + 1) * 128]),
                        writes=[wbs[1]], dma=True, lane=wl + "u")
                else:
                    lop = P.op("sp", lambda e, fc=fc, wt=wt: e.dma_start(
                        out=wt[:, 0:KC * 256], in_=pc["gu_b"][fc * 128:(fc + 1) * 128, :]),
                        reads=pcreads(), writes=[wbs[0], wbs[1]], dma=True, lane=wl)
                pacer.tick(last_mm[0])
                pg, pu = (0, 1) if fc % 2 == 0 else (2, 3)
                for half, pbi in ((0, pg), (1, pu)):
                    for kc in range(KC):
                        last_mm[0] = P.op("pe", lambda e, kc=kc, wv=wv, half=half, pbi=pbi: e.matmul(
                            pbank[pbi][:, 0:T], lhsT=wv[:, kc, half * 128:(half + 1) * 128], rhs=xT[:, kc, :],
                            start=(kc == 0), stop=(kc == KC - 1)),
                            reads=[wbs[half], b_xT], writes=[b_pb[pbi]])
                sgt, sgb, _ = sgr.next()
                P.op("act", lambda e, sgt=sgt, pg=pg: e.activation(out=sgt[:], in_=pbank[pg][:, 0:T], func=AF.Silu),
                     reads=[b_pb[pg]], writes=[sgb])
                P.op("dve", lambda e, sgt=sgt, pu=pu, fc=fc: e.tensor_tensor(
                    out=AT[:, fc, :], in0=sgt[:], in1=pbank[pu][:, 0:T], op=ALU.mult),
                    reads=[sgb, b_pb[pu]], writes=[b_AT[fc]])
            dview = wd if pc is None else None
            per = KC // DG

            def between(dg, tt=tt):
                if tt + 1 < NT:
                    load_stage(zi, tt + 1, ln, kcs=range(dg * per, (dg + 1) * per))

            down_like(AT, b_AT, NF, dview, zo, tt, 0.5,
                      pcsrc=None if pc is None else pc["dn_b"], pcreads=pcreads, between=between)

    TWO_PI = 2.0 * math.pi
    SCALE = 128 ** -0.5

    last_mm = [None]

    def down_like(src_t, src_bufs, nch, wview, zo, tt, res_scale, pcsrc=None, pcreads=None, between=None):
        for dg in range(D // 256):
            pb0 = 4 if dg % 2 == 0 else 2
            pres = [epilogue_prefetch(tt, dg * 2 + j) for j in range(2)]
            c0 = 0
            while c0 < nch:
                c1 = min(nch, c0 + 32)
                wt, wbs, wl = ring.next()
                wv = wt[:, 0:32 * 256].rearrange("p (c d) -> p c d", c=32)
                if pcsrc is None:
                    P.op("pool", lambda e, wv=wv, c0=c0, c1=c1, dg=dg: e.dma_start(
                        out=wv[:, 0:c1 - c0, :], in_=wview[:, c0:c1, dg * 256:(dg + 1) * 256]),
                        reads=[wbs[1]], writes=[wbs[0], wbs[1]], dma=True, lane=wl)
                else:
                    P.op("sp", lambda e, wv=wv, c0=c0, c1=c1, dg=dg: e.dma_start(
                        out=wv[:, 0:c1 - c0, :],
                        in_=pcsrc[dg * 128:(dg + 1) * 128, :].rearrange("p (c d) -> p c d", c=nch)[:, c0:c1, :]),
                        reads=[wbs[1]] + pcreads(), writes=[wbs[0], wbs[1]], dma=True, lane=wl)
                pacer.tick(last_mm[0])
                for c in range(c0, c1):
                    for j in range(2):
                        last_mm[0] = P.op("pe", lambda e, wv=wv, c=c, c0=c0, j=j, pb0=pb0: e.matmul(
                            pbank[pb0 + j][:, 0:T], lhsT=wv[:, c - c0, j * 128:(j + 1) * 128], rhs=src_t[:, c, :],
                            start=(c == 0), stop=(c == nch - 1)),
                            reads=[wbs[0], src_bufs[c]], writes=[b_pb[pb0 + j]])
                c0 = c1
            for j in range(2):
                epilogue(zo, tt, dg * 2 + j, pb0 + j, res_scale, pres[j])
            if between is not None:
                between(dg)

    def mix_chunks():
        feat_chunks = []
        for i in range(8):
            feat_chunks.append((3088 + i * 128, "rope", i))
        for i in range(2):
            feat_chunks.append((4112 + i * 128, "rope", 8 + i))
        for i in range(8):
            feat_chunks.append((5648 + i * 128, "rope", 18 + i))
        for i in range(8):
            feat_chunks.append((4624 + i * 128, "rope", 10 + i))
        for i in range(8):
            feat_chunks.append((7696 + i * 128, "plain", 26 + i))
        for i in range(8):
            feat_chunks.append((8720 + i * 128, "plain", 34 + i))
        for i in range(4):
            feat_chunks.append((0 + i * 128, "f32", i))
        for i in range(4):
            feat_chunks.append((512 + i * 128, "f32", 4 + i))
        for i in range(8):
            feat_chunks.append((2048 + i * 128, "f32", 8 + i))
        feat_chunks.append((3072, "lr", 0))
        tok_groups = []
        for i in range(4):
            tok_groups.append((1024 + i * 256, i * 256))
        tok_groups.append((4368, 1024))
        for i in range(4):
            tok_groups.append((6672 + i * 256, 1280 + i * 256))
        for i in range(4):
            tok_groups.append((9744 + i * 256, 2304 + i * 256))
        return feat_chunks, tok_groups

    MIX_STEPS = NT * (60 + 15 + DG)

    def mk_precast_mix(l):
        feat_chunks, tok_groups = mix_chunks()
        nfc = len(feat_chunks)
        winf_b = dscr("winf_%d" % l, [nfc * 128, KC * 128], BF16)
        wint_b = dscr("wint_%d" % l, [15 * 128, KC * 256], BF16)
        wo_b = dscr("wo_%d" % l, [DG * 128, KC * 256], BF16)
        win_v = w_in[l * D:(l + 1) * D, :].rearrange("(k p) f -> p k f", p=128)
        wout_v = w_out[l * D:(l + 1) * D, :].rearrange("(c p) d -> p c d", p=128)
        ops, bufs = [], []
        for i, (col0, kind, drow) in enumerate(feat_chunks):
            ncol = 16 if kind == "lr" else 128
            b = Buf("pc")
            bufs.append(b)
            ops.append((lambda e, i=i, col0=col0, ncol=ncol: e.dma_start(
                out=winf_b[i * 128:(i + 1) * 128, 0:KC * ncol].rearrange("p (k c) -> p k c", k=KC),
                in_=win_v[:, :, col0:col0 + ncol]), b))
        tcols = [c for (c, _) in tok_groups] + [512, 768]
        for j, col0 in enumerate(tcols):
            b = Buf("pc")
            bufs.append(b)
            ops.append((lambda e, j=j, col0=col0: e.dma_start(
                out=wint_b[j * 128:(j + 1) * 128, :].rearrange("p (k c) -> p k c", k=KC),
                in_=win_v[:, :, col0:col0 + 256]), b))
        for dg in range(DG):
            b = Buf("pc")
            bufs.append(b)
            ops.append((lambda e, dg=dg: e.dma_start(
                out=wo_b[dg * 128:(dg + 1) * 128, :].rearrange("p (c d) -> p c d", c=KC),
                in_=wout_v[:, :, dg * 256:(dg + 1) * 256]), b))
        return dict(winf_b=winf_b, wint_b=wint_b, wo_b=wo_b, ops=ops, bufs=bufs, lane="pcm%d" % l)

    def mixer(l, zi, zo, ln, pc=None):
        P.barrier()
        AQ[0] = "sp" if pc is None else "pool"
        first = [True]

        def pcreads():
            if pc is not None and first[0]:
                first[0] = False
                return list(pc["bufs"])
            return []

        carve_reset()
        cos2 = carve([128, S], F32)
        sinS = carve([128, S], F32)
        rtmp = carve([128, S], F32)
        rk = carve([128, S], F32)
        qf_t = [carve([128, T], F32) for _ in range(2)]
        qfr = Ring([(qf_t[i], Buf("qf%d" % i), None) for i in range(2)])
        b_tab = Buf("ropetab")
        win_v = w_in[l * D:(l + 1) * D, :].rearrange("(k p) f -> p k f", p=128)
        wout_v = w_out[l * D:(l + 1) * D, :].rearrange("(c p) d -> p c d", p=128)

        posi = rk.bitcast(I32)
        P.op("sp", lambda e: e.dma_start(out=posi, in_=pos_in.partition_broadcast(128)),
             writes=[b_tab], dma=True, lane="tab")
        P.op("dve", lambda e: e.tensor_copy(out=rtmp, in_=posi), reads=[b_tab], writes=[b_tab])
        P.op("dve", lambda e: e.tensor_scalar(out=rtmp, in0=rtmp, scalar1=inv2[:, 0:1], scalar2=None, op0=ALU.mult),
             reads=[b_tab, b_const], writes=[b_tab])

        def reduce_sin(dst, shift):
            ki = rk.bitcast(I32)
            P.op("dve", lambda e: e.tensor_scalar(out=dst, in0=rtmp, scalar1=shift, scalar2=1.0 / TWO_PI,
                                                  op0=ALU.add, op1=ALU.mult), reads=[b_tab], writes=[b_tab])
            P.op("dve", lambda e: e.tensor_copy(out=ki, in_=dst), reads=[b_tab], writes=[b_tab])
            P.op("dve", lambda e: e.tensor_copy(out=dst, in_=ki), reads=[b_tab], writes=[b_tab])
            P.op("dve", lambda e: e.tensor_scalar(out=dst, in0=dst, scalar1=-TWO_PI, scalar2=shift,
                                                  op0=ALU.mult, op1=ALU.add), reads=[b_tab], writes=[b_tab])
            P.op("dve", lambda e: e.tensor_tensor(out=dst, in0=dst, in1=rtmp, op=ALU.add),
                 reads=[b_tab], writes=[b_tab])
            P.op("dve", lambda e: e.tensor_scalar(out=rk, in0=dst, scalar1=-math.pi, scalar2=TWO_PI,
                                                  op0=ALU.is_lt, op1=ALU.mult), reads=[b_tab], writes=[b_tab])
            P.op("dve", lambda e: e.tensor_tensor(out=dst, in0=dst, in1=rk, op=ALU.add),
                 reads=[b_tab], writes=[b_tab])
            P.op("dve", lambda e: e.tensor_scalar(out=rk, in0=dst, scalar1=math.pi, scalar2=-TWO_PI,
                                                  op0=ALU.is_gt, op1=ALU.mult), reads=[b_tab], writes=[b_tab])
            P.op("dve", lambda e: e.tensor_tensor(out=dst, in0=dst, in1=rk, op=ALU.add),
                 reads=[b_tab], writes=[b_tab])
            P.op("dve", lambda e: e.tensor_scalar(out=dst, in0=dst, scalar1=-3.14159, scalar2=3.14159,
                                                  op0=ALU.max, op1=ALU.min), reads=[b_tab], writes=[b_tab])
            P.op("act", lambda e: e.activation(out=dst, in_=dst, func=AF.Sin), reads=[b_tab], writes=[b_tab])

        reduce_sin(cos2, math.pi / 2)
        reduce_sin(sinS, 0.0)
        P.op("dve", lambda e: e.tensor_scalar(out=sinS, in0=sinS, scalar1=sgn2[:, 0:1], scalar2=None, op0=ALU.mult),
             reads=[b_tab, b_const], writes=[b_tab])

        feat_chunks, tok_groups = mix_chunks()
        pbi = 0
        for tt in range(NT):
            t0 = tt * T
            load_stage(zi, tt, ln, deep=True)
            for ci, (col0, kind, drow) in enumerate(feat_chunks):
                wt, wbs, wl = ring.next()
                ncol = 16 if kind == "lr" else 128
                wv = wt[:, 0:KC * ncol].rearrange("p (k c) -> p k c", k=KC)
                if pc is None:
                    lop = P.op("pool", lambda e, wv=wv, col0=col0, ncol=ncol: e.dma_start(
                        out=wv, in_=win_v[:, :, col0:col0 + ncol]),
                        reads=[wbs[1]], writes=[wbs[0], wbs[1]], dma=True, lane=wl)
                else:
                    lop = P.op("sp", lambda e, wt=wt, ci=ci, ncol=ncol: e.dma_start(
                        out=wt[:, 0:KC * ncol], in_=pc["winf_b"][ci * 128:(ci + 1) * 128, 0:KC * ncol]),
                        reads=[wbs[1]] + pcreads(), writes=[wbs[0], wbs[1]], dma=True, lane=wl)
                pacer.tick(last_mm[0])
                pb = pbi % 2
                pbi += 1
                for kc in range(KC):
                    last_mm[0] = P.op("pe", lambda e, kc=kc, wv=wv, pb=pb, ncol=ncol: e.matmul(
                        pbank[pb][0:ncol, 0:T], lhsT=wv[:, kc, :], rhs=xT[:, kc, :], start=(kc == 0), stop=(kc == KC - 1)),
                        reads=[wbs[0], b_xT], writes=[b_pb[pb]])
                if kind in ("f32", "lr"):
                    sx, sxb, sl = stg.next()
                    P.op("act", lambda e, sx=sx, pb=pb, ncol=ncol: e.activation(
                        out=sx[0:ncol, :], in_=pbank[pb][0:ncol, 0:T], func=AF.Copy),
                         reads=[b_pb[pb]], writes=[sxb])
                    if kind == "f32":
                        P.op(AQ[0], lambda e, sx=sx, drow=drow, t0=t0: e.dma_start(
                            out=gfT[drow * 128:(drow + 1) * 128, t0:t0 + T], in_=sx[:]),
                            reads=[sxb], writes=[b_gf], dma=True, lane=sl)
                    else:
                        P.op(AQ[0], lambda e, sx=sx, t0=t0: e.dma_start(out=glrT[:, t0:t0 + T], in_=sx[0:16, :]),
                             reads=[sxb], writes=[b_glr], dma=True, lane=sl)
                    continue
                zq, zqb, _ = zbr.next()
                if kind == "plain":
                    P.op("act", lambda e, zq=zq, pb=pb: e.activation(out=zq[:, 0, :], in_=pbank[pb][:, 0:T], func=AF.Copy),
                         reads=[b_pb[pb]], writes=[zqb])
                else:
                    qf, qfb, _ = qfr.next()
                    P.op("act", lambda e, qf=qf, pb=pb: e.activation(out=qf, in_=pbank[pb][:, 0:T], func=AF.Copy),
                         reads=[b_pb[pb]], writes=[qfb])
                    P.op("pe", lambda e, qf=qf, pb=pb: e.matmul(pbank[2 + pb][:, 0:T], lhsT=perm_f[:], rhs=qf,
                                                                start=True, stop=True),
                         reads=[qfb, b_const], writes=[b_pb[2 + pb]])
                    P.op("dve", lambda e, pb=pb, t0=t0: e.tensor_tensor(out=tmp_f[:], in0=pbank[2 + pb][:, 0:T],
                                                                        in1=sinS[:, t0:t0 + T], op=ALU.mult),
                         reads=[b_pb[2 + pb], b_tab], writes=[b_tmp])
                    P.op("dve", lambda e, qf=qf, t0=t0: e.tensor_tensor(out=qf, in0=qf, in1=cos2[:, t0:t0 + T], op=ALU.mult),
                         reads=[qfb, b_tab], writes=[qfb])
                    P.op("dve", lambda e, qf=qf, zq=zq: e.tensor_tensor(out=zq[:, 0, :], in0=qf, in1=tmp_f[:], op=ALU.add),
                         reads=[qfb, b_tmp], writes=[zqb])
                P.op(AQ[0], lambda e, zq=zq, drow=drow, t0=t0: e.dma_start(
                    out=featT[drow * 128:(drow + 1) * 128, t0:t0 + T], in_=zq[:, 0, :]),
                    reads=[zqb], writes=[b_feat], dma=True, lane="featst")
            for gj, (col0, dcol) in enumerate(tok_groups):
                wt, wbs, wl = ring.next()
                wv = wt[:, 0:KC * 256].rearrange("p (k c) -> p k c", k=KC)
                if pc is None:
                    lop = P.op("pool", lambda e, wv=wv, col0=col0: e.dma_start(out=wv, in_=win_v[:, :, col0:col0 + 256]),
                               reads=[wbs[1]], writes=[wbs[0], wbs[1]], dma=True, lane=wl)
                else:
                    lop = P.op("sp", lambda e, wt=wt, gj=gj: e.dma_start(
                        out=wt[:, 0:KC * 256], in_=pc["wint_b"][gj * 128:(gj + 1) * 128, :]),
                        reads=[wbs[1]] + pcreads(), writes=[wbs[0], wbs[1]], dma=True, lane=wl)
                pacer.tick(last_mm[0])
                for tc4 in range(T // 128):
                    pb = pbi % 2
                    pbi += 1
                    for kc in range(KC):
                        last_mm[0] = P.op("pe", lambda e, kc=kc, wv=wv, pb=pb, tc4=tc4: e.matmul(
                            pbank[pb][:, 0:256], lhsT=xT[:, kc, tc4 * 128:(tc4 + 1) * 128], rhs=wv[:, kc, :],
                            start=(kc == 0), stop=(kc == KC - 1)),
                            reads=[wbs[0], b_xT], writes=[b_pb[pb]])
                    zq, zqb, _ = zbr.next()
                    P.op("act", lambda e, zq=zq, pb=pb: e.activation(out=zq[:, 0, 0:256], in_=pbank[pb][:, 0:256], func=AF.Copy),
                         reads=[b_pb[pb]], writes=[zqb])
                    P.op(AQ[0], lambda e, zq=zq, dcol=dcol, tc4=tc4, t0=t0: e.dma_start(
                        out=tokM[t0 + tc4 * 128:t0 + (tc4 + 1) * 128, dcol:dcol + 256], in_=zq[:, 0, 0:256]),
                        reads=[zqb], writes=[b_tok], dma=True, lane="tokst")

            for gi in range(2):
                col0 = 512 + gi * 256
                wt, wbs, wl = ring.next()
                wv = wt[:, 0:KC * 256].rearrange("p (k c) -> p k c", k=KC)
                if pc is None:
                    lop = P.op("pool", lambda e, wv=wv, col0=col0: e.dma_start(out=wv, in_=win_v[:, :, col0:col0 + 256]),
                               reads=[wbs[1]], writes=[wbs[0], wbs[1]], dma=True, lane=wl)
                else:
                    lop = P.op("sp", lambda e, wt=wt, gi=gi: e.dma_start(
                        out=wt[:, 0:KC * 256], in_=pc["wint_b"][(13 + gi) * 128:(14 + gi) * 128, :]),
                        reads=[wbs[1]] + pcreads(), writes=[wbs[0], wbs[1]], dma=True, lane=wl)
                pacer.tick(last_mm[0])
                for tc4 in range(T // 128):
                    pb = pbi % 2
                    pbi += 1
                    for kc in range(KC):
                        last_mm[0] = P.op("pe", lambda e, kc=kc, wv=wv, pb=pb, tc4=tc4: e.matmul(
                            pbank[pb][:, 0:256], lhsT=xT[:, kc, tc4 * 128:(tc4 + 1) * 128], rhs=wv[:, kc, :],
                            start=(kc == 0), stop=(kc == KC - 1)),
                            reads=[wbs[0], b_xT], writes=[b_pb[pb]])
                    sx, sxb, sl = stg.next()
                    P.op("act", lambda e, sx=sx, pb=pb: e.activation(out=sx[:, 0:256], in_=pbank[pb][:, 0:256], func=AF.Copy),
                         reads=[b_pb[pb]], writes=[sxb])
                    P.op(AQ[0], lambda e, sx=sx, gi=gi, tc4=tc4, t0=t0: e.dma_start(
                        out=gktok[t0 + tc4 * 128:t0 + (tc4 + 1) * 128, gi * 256:(gi + 1) * 256], in_=sx[:, 0:256]),
                        reads=[sxb], writes=[b_gkt], dma=True, lane=sl)

        P.barrier()
        carve_reset()
        esink2 = carve([128, 8], F32)
        b_es = Buf("esink2")
        P.op("sp", lambda e: e.dma_start(out=esink2, in_=sinks_in[l:l + 1, :].partition_broadcast(128)),
             writes=[b_es], dma=True, lane="tab")
        P.op("act", lambda e: e.activation(out=esink2, in_=esink2, func=AF.Exp), reads=[b_es], writes=[b_es])
        zero_t = carve([128, S], BF16)
        b_zero = Buf("zero")
        P.op("dve", lambda e: e.memset(zero_t, 0.0), writes=[b_zero])
        for rows in ZERO_Y_ROWS:
            P.op("sp", lambda e, rows=rows: e.dma_start(out=yT[rows * 128:(rows + 1) * 128, :], in_=zero_t),
                 reads=[b_zero], writes=[b_yT], dma=True, lane="yst")
        featT_v = featT.rearrange("(c p) s -> p c s", p=128)
        tokM_v = tokM.rearrange("(c p) f -> p c f", p=128)
        yT_v = yT.rearrange("(c p) s -> p c s", p=128)
        NQB = S // 128
        kt_t = carve([128, S], BF16)
        b_kt = Buf("swa_kt")
        vt_t = carve([128, NQB, 128], BF16)
        b_vt = Buf("swa_vt")
        q4_t = carve([128, 4, S], BF16)
        b_q4 = Buf("swa_q4")
        pt_t = [carve([128, 512], BF16) for _ in range(4)]
        ptr = Ring([(pt_t[i], Buf("pt%d" % i), None) for i in range(4)])
        den_t = carve([128, 512], F32)
        b_den = Buf("den")
        yo_t = [carve([128, 512], BF16) for _ in range(2)]
        yor = Ring([(yo_t[i], Buf("yo%d" % i), "yo%d" % i) for i in range(2)])
        for kv in range(2):
            P.op("sp", lambda e, kv=kv: e.dma_start(out=kt_t, in_=featT[(8 + kv) * 128:(9 + kv) * 128, :]),
                 reads=[b_feat], writes=[b_kt], dma=True, lane="swak")
            P.op("sp", lambda e, kv=kv: e.dma_start(out=vt_t, in_=tokM_v[:, :, 1024 + kv * 128:1024 + (kv + 1) * 128]),
                 reads=[b_tok], writes=[b_vt], dma=True, lane="swav")
            P.op("sp", lambda e, kv=kv: e.dma_start(out=q4_t, in_=featT_v[:, kv * 4:kv * 4 + 4, :]),
                 reads=[b_feat], writes=[b_q4], dma=True, lane="swaq")
            for n in range(NQB):
                pts = []
                for which in (("cur", n, mcur), ("prev", n - 1, mprev)):
                    nm, kc_, msk = which
                    if kc_ < 0:
                        continue
                    pb = 0 if nm == "cur" else 1
                    P.op("pe", lambda e, kc_=kc_, n=n, pb=pb: e.matmul(
                        pbank[pb][:, 0:512], lhsT=kt_t[:, kc_ * 128:(kc_ + 1) * 128],
                        rhs=q4_t[:, :, n * 128:(n + 1) * 128], start=True, stop=False),
                        reads=[b_kt, b_q4], writes=[b_pb[pb]])
                    P.op("pe", lambda e, msk=msk, pb=pb: e.matmul(
                        pbank[pb][:, 0:512], lhsT=ident_b[:], rhs=msk[:], start=False, stop=True),
                        reads=[b_const], writes=[b_pb[pb]])
                    pt, ptb, _ = ptr.next()
                    P.op("act", lambda e, pt=pt, pb=pb: e.activation(out=pt, in_=pbank[pb][:, 0:512], func=AF.Exp, scale=SCALE),
                         reads=[b_pb[pb]], writes=[ptb])
                    pts.append((pt, ptb, kc_))
                npts = len(pts)
                for i_, (pt, ptb, kc_) in enumerate(pts):
                    P.op("pe", lambda e, pt=pt, kc_=kc_, i_=i_, npts=npts: e.matmul(
                        pbank[2][:, 0:512], lhsT=vt_t[:, kc_, :], rhs=pt, start=(i_ == 0), stop=(i_ == npts - 1)),
                        reads=[ptb, b_vt], writes=[b_pb[2]])
                for i_, (pt, ptb, kc_) in enumerate(pts):
                    P.op("pe", lambda e, pt=pt, i_=i_, npts=npts: e.matmul(
                        pbank[3][:, 0:512], lhsT=ones_b[:], rhs=pt, start=(i_ == 0), stop=(i_ == npts - 1)),
                        reads=[ptb, b_const], writes=[b_pb[3]])
                for g in range(4):
                    P.op("dve", lambda e, g=g, kv=kv: e.tensor_scalar(
                        out=den_t[:, g * 128:(g + 1) * 128], in0=pbank[3][:, g * 128:(g + 1) * 128],
                        scalar1=esink2[:, kv * 4 + g:kv * 4 + g + 1], scalar2=None, op0=ALU.add),
                        reads=[b_pb[3], b_es], writes=[b_den])
                P.op("dve", lambda e: e.reciprocal(out=den_t, in_=den_t), reads=[b_den], writes=[b_den])
                yo, yob, yol = yor.next()
                P.op("dve", lambda e, yo=yo: e.tensor_tensor(out=yo, in0=pbank[2][:, 0:512], in1=den_t, op=ALU.mult),
                     reads=[b_pb[2], b_den], writes=[yob])
                P.op("sp", lambda e, yo=yo, kv=kv, n=n: e.dma_start(
                    out=yT_v[:, 8 + kv * 4:8 + kv * 4 + 4, n * 128:(n + 1) * 128],
                    in_=yo.rearrange("p (g t) -> p g t", g=4)),
                    reads=[yob], writes=[b_yT], dma=True, lane=yol)

        if "sb" in MIXERS:
            P.barrier()
            carve_reset()
            msb = carve([128, 5, 512], F32)
            ustr = carve([128, 128], F32)
            onesf = carve([128, 128], F32)
            b_sc = Buf("sb_consts")
            P.op("sp", lambda e: e.dma_start(out=msb, in_=c_msb5.rearrange("p (m t) -> p m t", m=5)),
                 writes=[b_sc], dma=True, lane="sbc")
            P.op("sp", lambda e: e.dma_start(out=ustr, in_=c_ustrict), writes=[b_sc], dma=True, lane="sbc")
            P.op("sp", lambda e: e.dma_start(out=onesf, in_=c_ones_f), writes=[b_sc], dma=True, lane="sbc")
            ctxs = []
            for ci in range(2):
                cx = dict(
                    q=carve([128, S], BF16), k=carve([128, S], BF16), v=carve([128, NQB, 128], BF16),
                    bq=Buf("sb_q%d" % ci), bk=Buf("sb_k%d" % ci), bv=Buf("sb_v%d" % ci),
                    sp=carve([128, 512], F32), lm=carve([128, 512], F32), rs=carve([128, 512], F32),
                    t1=carve([128, 512], F32),
                    bsp=Buf("sb_sp%d" % ci), blm=Buf("sb_lm%d" % ci), brs=Buf("sb_rs%d" % ci), bt1=Buf("sb_t1%d" % ci),
                    pz=3 * ci, pl=3 * ci + 1, po=3 * ci + 2, ci=ci)
                wb_t = [carve([128, 512], BF16) for _ in range(2)]
                cx["wbr"] = Ring([(wb_t[i], Buf("sbw%d_%d" % (ci, i)), None) for i in range(2)])
                yo_s = [carve([128, 512], BF16) for _ in range(2)]
                cx["yor"] = Ring([(yo_s[i], Buf("sbyo%d_%d" % (ci, i)), "sbyo%d_%d" % (ci, i)) for i in range(2)])
                ctxs.append(cx)

            def sb_load(cx, h):
                P.op("sp", lambda e: e.dma_start(out=cx["q"], in_=featT[(26 + h) * 128:(27 + h) * 128, :]),
                     reads=[b_feat], writes=[cx["bq"]], dma=True, lane="sbq%d" % cx["ci"])
                P.op("sp", lambda e: e.dma_start(out=cx["k"], in_=featT[(34 + h) * 128:(35 + h) * 128, :]),
                     reads=[b_feat], writes=[cx["bk"]], dma=True, lane="sbk%d" % cx["ci"])
                P.op("sp", lambda e: e.dma_start(out=cx["v"], in_=tokM_v[:, :, 2304 + h * 128:2304 + (h + 1) * 128]),
                     reads=[b_tok], writes=[cx["bv"]], dma=True, lane="sbv%d" % cx["ci"])

            def sb_step(cx, h, qg, ki, kc, nk):
                q0 = qg * 512
                mi = kc - 4 * qg if kc >= 4 * qg else 4
                pz, pl, po = cx["pz"], cx["pl"], cx["po"]
                sp_t, lm_t, rs_t, t1_t = cx["sp"], cx["lm"], cx["rs"], cx["t1"]
                b_sp, b_lm, b_rs, b_t1 = cx["bsp"], cx["blm"], cx["brs"], cx["bt1"]
                P.op("pe", lambda e: e.matmul(
                    pbank[pz][:, 0:512], lhsT=cx["k"][:, kc * 128:(kc + 1) * 128], rhs=cx["q"][:, q0:q0 + 512],
                    start=True, stop=True), reads=[cx["bk"], cx["bq"]], writes=[b_pb[pz]])
                P.op("act", lambda e: e.activation(out=sp_t, in_=pbank[pz][:, 0:512], func=AF.Exp, scale=SCALE),
                     reads=[b_pb[pz]], writes=[b_sp])
                P.op("act", lambda e: e.activation(out=sp_t, in_=sp_t, func=AF.Ln, bias=1.0),
                     reads=[b_sp], writes=[b_sp])
                P.op("dve", lambda e: e.scalar_tensor_tensor(
                    out=lm_t, in0=sp_t, scalar=-1.0, in1=msb[:, mi, :], op0=ALU.mult, op1=ALU.mult),
                    reads=[b_sp, b_sc], writes=[b_lm])
                P.op("pe", lambda e: e.matmul(pbank[pl][:, 0:512], lhsT=ustr, rhs=lm_t, start=True, stop=(ki == 0)),
                     reads=[b_lm, b_sc], writes=[b_pb[pl]])
                if ki > 0:
                    P.op("pe", lambda e: e.matmul(pbank[pl][:, 0:512], lhsT=onesf, rhs=rs_t, start=False, stop=True),
                         reads=[b_rs, b_sc], writes=[b_pb[pl]])
                P.op("dve", lambda e: e.scalar_tensor_tensor(
                    out=t1_t, in0=pbank[pz][:, 0:512], scalar=SCALE, in1=sp_t, op0=ALU.mult, op1=ALU.subtract),
                    reads=[b_pb[pz], b_sp], writes=[b_t1])
                P.op("dve", lambda e: e.tensor_tensor(out=t1_t, in0=t1_t, in1=pbank[pl][:, 0:512], op=ALU.add),
                     reads=[b_t1, b_pb[pl]], writes=[b_t1])
                P.op("act", lambda e: e.activation(out=t1_t, in_=t1_t, func=AF.Exp), reads=[b_t1], writes=[b_t1])
                wbt, wbb, _ = cx["wbr"].next()
                P.op("dve", lambda e: e.tensor_tensor(out=wbt, in0=t1_t, in1=msb[:, mi, :], op=ALU.mult),
                     reads=[b_t1, b_sc], writes=[wbb])
                P.op("pe", lambda e: e.matmul(
                    pbank[po][:, 0:512], lhsT=cx["v"][:, kc, :], rhs=wbt, start=(ki == 0), stop=(ki == nk - 1)),
                    reads=[wbb, cx["bv"]], writes=[b_pb[po]])
                if ki == 0:
                    P.op("dve", lambda e: e.tensor_copy(out=rs_t, in_=lm_t), reads=[b_lm], writes=[b_rs])
                else:
                    P.op("dve", lambda e: e.tensor_tensor(out=rs_t, in0=rs_t, in1=lm_t, op=ALU.add),
                         reads=[b_lm, b_rs], writes=[b_rs])
                if ki == nk - 1:
                    yo, yob, yol = cx["yor"].next()
                    P.op("act", lambda e: e.activation(out=yo, in_=pbank[po][:, 0:512], func=AF.Copy),
                         reads=[b_pb[po]], writes=[yob])
                    P.op("sp", lambda e: e.dma_start(out=yT[(24 + h) * 128:(25 + h) * 128, q0:q0 + 512], in_=yo),
                         reads=[yob], writes=[b_yT], dma=True, lane=yol)

            for hp in range(4):
                for cx in ctxs:
                    sb_load(cx, 2 * hp + cx["ci"])
                for qg in range(S // 512):
                    klist = list(range(4 * qg + 3, -1, -1))
                    for ki, kc in enumerate(klist):
                        for cx in ctxs:
                            sb_step(cx, 2 * hp + cx["ci"], qg, ki, kc, len(klist))

        if "moba" in MIXERS:
            P.barrier()
            carve_reset()
            NB = S // 256
            e8_t = carve([8, 1024], BF16)
            pmask = carve([128, 64], F32)
            mown_t = carve([128, 2, 256], BF16)
            b_mc = Buf("moba_consts")
            P.op("sp", lambda e: e.dma_start(out=e8_t, in_=c_e8), writes=[b_mc], dma=True, lane="mbc")
            P.op("sp", lambda e: e.dma_start(out=pmask, in_=c_pastmask), writes=[b_mc], dma=True, lane="mbc")
            P.op("sp", lambda e: e.dma_start(out=mown_t, in_=c_mown.rearrange("p (m t) -> p m t", m=2)),
                 writes=[b_mc], dma=True, lane="mbc")
            qT_m = carve([128, S], BF16)
            kT_m = carve([128, S], BF16)
            v_m = carve([128, NQB, 128], BF16)
            b_qm, b_km, b_vm = Buf("mb_q"), Buf("mb_k"), Buf("mb_v")
            kbar_f = carve([128, 8], F32)
            kbar_b = carve([128, 8], BF16)
            b_kb = Buf("kbar")
            gm_t = carve([128, 8], F32)
            top_t = carve([128, 8], F32)
            bias_f = carve([128, 8], F32)
            b_gm = Buf("gm")
            biasT = carve([8, S], BF16)
            b_bT = Buf("biasT")
            pm_t = [carve([128, 256], BF16) for _ in range(3)]
            pmr = Ring([(pm_t[i], Buf("mbp%d" % i), None) for i in range(3)])
            den_m = carve([128, 256], F32)
            b_dm = Buf("mb_den")
            yo_m = [carve([128, 256], BF16) for _ in range(2)]
            yomr = Ring([(yo_m[i], Buf("mbyo%d" % i), "mbyo%d" % i) for i in range(2)])
            for h in range(8):
                P.op("sp", lambda e, h=h: e.dma_start(out=qT_m, in_=featT[(10 + h) * 128:(11 + h) * 128, :]),
                     reads=[b_feat], writes=[b_qm], dma=True, lane="mbq")
                P.op("sp", lambda e, h=h: e.dma_start(out=kT_m, in_=featT[(18 + h) * 128:(19 + h) * 128, :]),
                     reads=[b_feat], writes=[b_km], dma=True, lane="mbk")
                P.op("sp", lambda e, h=h: e.dma_start(out=v_m, in_=tokM_v[:, :, 1280 + h * 128:1280 + (h + 1) * 128]),
                     reads=[b_tok], writes=[b_vm], dma=True, lane="mbv")
                P.op("dve", lambda e: e.memset(kbar_f, 0.0), writes=[b_kb])
                P.op("dve", lambda e: e.tensor_reduce(out=kbar_f[:, 0:NB], in_=kT_m.rearrange("p (n t) -> p n t", t=256),
                                                      axis=AX.X, op=ALU.add), reads=[b_km], writes=[b_kb])
                P.op("dve", lambda e: e.tensor_copy(out=kbar_b, in_=kbar_f), reads=[b_kb], writes=[b_kb])
                for qc in range(NQB):
                    own = qc // 2
                    P.op("pe", lambda e, qc=qc: e.matmul(pbank[0][:, 0:8], lhsT=qT_m[:, qc * 128:(qc + 1) * 128],
                                                         rhs=kbar_b, start=True, stop=True),
                         reads=[b_qm, b_kb], writes=[b_pb[0]])
                    P.op("dve", lambda e, own=own: e.tensor_tensor(out=gm_t, in0=pbank[0][:, 0:8],
                                                                   in1=pmask[:, own * 8:(own + 1) * 8], op=ALU.add),
                         reads=[b_pb[0], b_mc], writes=[b_gm])
                    P.op("dve", lambda e: e.max(out=top_t, in_=gm_t), reads=[b_gm], writes=[b_gm])
                    P.op("dve", lambda e: e.tensor_scalar(out=bias_f, in0=gm_t, scalar1=top_t[:, 2:3], scalar2=NEG,
                                                          op0=ALU.is_lt, op1=ALU.mult), reads=[b_gm], writes=[b_gm])
                    P.op("pe", lambda e: e.transpose(out=pbank[1][0:8, 0:128], in_=bias_f, identity=ident_f[:]),
                         reads=[b_gm, b_const], writes=[b_pb[1]])
                    P.op("act", lambda e, qc=qc: e.activation(out=biasT[:, qc * 128:(qc + 1) * 128],
                                                              in_=pbank[1][0:8, 0:128], func=AF.Copy),
                         reads=[b_pb[1]], writes=[b_bT])
                for b in range(NB):
                    q0 = b * 256
                    nk = 2 * b + 2
                    for kc in range(nk):
                        nb_ = kc // 2
                        sbk = 2 if kc % 2 == 0 else 4
                        P.op("pe", lambda e, kc=kc, q0=q0, sbk=sbk: e.matmul(
                            pbank[sbk][:, 0:256], lhsT=kT_m[:, kc * 128:(kc + 1) * 128], rhs=qT_m[:, q0:q0 + 256],
                            start=True, stop=False), reads=[b_km, b_qm], writes=[b_pb[sbk]])
                        if nb_ < b:
                            P.op("pe", lambda e, nb_=nb_, q0=q0, sbk=sbk: e.matmul(
                                pbank[sbk][:, 0:256], lhsT=e8_t[:, nb_ * 128:(nb_ + 1) * 128], rhs=biasT[:, q0:q0 + 256],
                                start=False, stop=True), reads=[b_mc, b_bT], writes=[b_pb[sbk]])
                        else:
                            P.op("pe", lambda e, m_=kc % 2, sbk=sbk: e.matmul(
                                pbank[sbk][:, 0:256], lhsT=ident_b[:], rhs=mown_t[:, m_, :],
                                start=False, stop=True), reads=[b_mc, b_const], writes=[b_pb[sbk]])
                        pm, pmb, _ = pmr.next()
                        P.op("act", lambda e, pm=pm, sbk=sbk: e.activation(out=pm, in_=pbank[sbk][:, 0:256], func=AF.Exp, scale=SCALE),
                             reads=[b_pb[sbk]], writes=[pmb])
                        P.op("pe", lambda e, pm=pm, kc=kc, nk=nk: e.matmul(
                            pbank[3][:, 0:256], lhsT=v_m[:, kc, :], rhs=pm, start=(kc == 0), stop=(kc == nk - 1)),
                            reads=[pmb, b_vm], writes=[b_pb[3]])
                        P.op("pe", lambda e, pm=pm, kc=kc, nk=nk: e.matmul(
                            pbank[1][:, 0:256], lhsT=ones_b[:], rhs=pm, start=(kc == 0), stop=(kc == nk - 1)),
                            reads=[pmb, b_const], writes=[b_pb[1]])
                    P.op("dve", lambda e: e.reciprocal(out=den_m, in_=pbank[1][:, 0:256]), reads=[b_pb[1]], writes=[b_dm])
                    yo, yob, yol = yomr.next()
                    P.op("dve", lambda e, yo=yo: e.tensor_tensor(out=yo, in0=pbank[3][:, 0:256], in1=den_m, op=ALU.mult),
                         reads=[b_pb[3], b_dm], writes=[yob])
                    P.op("sp", lambda e, yo=yo, h=h, q0=q0: e.dma_start(
                        out=yT[(16 + h) * 128:(17 + h) * 128, q0:q0 + 256], in_=yo),
                        reads=[yob], writes=[b_yT], dma=True, lane=yol)

        if "gla" in MIXERS:
            P.barrier()
            carve_reset()
            tn_t = carve([128, 128], F32)
            un_t = carve([128, 128], F32)
            mtri_t = carve([128, 128], F32)
            ones1 = carve([128, 128], F32)
            wup_t = carve([16, 512], F32)
            bup_t = carve([1, 512], F32)
            ng_t = carve([128, 2], F32)
            lr_t = carve([16, S], F32)
            b_gc = Buf("gla_consts")
            for dst_, src_, kw in ((tn_t, c_tn, {}), (un_t, c_un, {}), (mtri_t, c_mtri, {}), (ones1, c_ones_f, {}),
                                   (wup_t, gla_wg[l * 16:(l + 1) * 16, :], {}), (bup_t, gla_bg[l:l + 1, :], {}),
                                   (ng_t, gla_ng[l:l + 1, :].rearrange("o (c p) -> p (o c)", p=128),
                                    dict(allow_slow_non_contiguous=True)),
                                   (lr_t, glrT, {})):
                P.op("sp", lambda e, dst_=dst_, src_=src_, kw=kw: e.dma_start(out=dst_, in_=src_, **kw),
                     reads=[b_glr], writes=[b_gc], dma=True, lane="glc")
            SQ = 128 ** -0.5
            q_g = carve([128, S], F32)
            k_g = carve([128, S], F32)
            kt_g = carve([128, NQB, 128], F32)
            v_g = carve([128, NQB, 256], BF16)
            o_g = carve([128, 2, S], F32)
            b_qg, b_kg, b_ktg, b_vg, b_og = Buf("g_q"), Buf("g_k"), Buf("g_kt"), Buf("g_v"), Buf("g_o")
            L_t = carve([128, 128], F32)
            eq_t = carve([128, 128], F32)
            ek_t = carve([128, 128], F32)
            er_t = carve([128, 128], F32)
            qt_b = carve([128, 128], BF16)
            kt_b = carve([128, 128], BF16)
            kh_b = carve([128, 128], BF16)
            at_b = carve([128, 128], BF16)
            S_f = carve([128, 256], F32)
            S_b = carve([128, 256], BF16)
            b_L, b_eq, b_ek, b_er = Buf("g_L"), Buf("g_eq"), Buf("g_ek"), Buf("g_er")
            b_qt, b_ktb, b_kh, b_at, b_S = Buf("g_qt"), Buf("g_ktb"), Buf("g_kh"), Buf("g_at"), Buf("g_S")
            gg_t = carve([128, 512], F32)
            b_gg = Buf("g_gate")
            rs_g = carve([128, 512], F32)
            b_rsg = Buf("g_rstd")
            sq_b = carve([128, 2, 512], BF16)
            b_sq = Buf("g_sq")
            yo_g = [carve([128, 512], BF16) for _ in range(2)]
            yogr = Ring([(yo_g[i], Buf("gyo%d" % i), "gyo%d" % i) for i in range(2)])
            gktok_v = gktok.rearrange("(c p) f -> p c f", p=128)
            for hd in range(4):
                P.op("sp", lambda e, hd=hd: e.dma_start(out=q_g, in_=gfT[hd * 128:(hd + 1) * 128, :]),
                     reads=[b_gf], writes=[b_qg], dma=True, lane="glq")
                P.op("sp", lambda e, hd=hd: e.dma_start(out=k_g, in_=gfT[(4 + hd) * 128:(5 + hd) * 128, :]),
                     reads=[b_gf], writes=[b_kg], dma=True, lane="glk")
                P.op("sp", lambda e, hd=hd: e.dma_start(out=kt_g, in_=gktok_v[:, :, hd * 128:(hd + 1) * 128]),
                     reads=[b_gkt], writes=[b_ktg], dma=True, lane="glkt")
                P.op("sp", lambda e, hd=hd: e.dma_start(out=v_g, in_=tokM_v[:, :, hd * 256:(hd + 1) * 256]),
                     reads=[b_tok], writes=[b_vg], dma=True, lane="glv")
                for c in range(NQB):
                    cs = slice(c * 128, (c + 1) * 128)
                    P.op("pe", lambda e, cs=cs, hd=hd: e.matmul(pbank[0][:, 0:128], lhsT=lr_t[:, cs],
                                                                rhs=wup_t[:, hd * 128:(hd + 1) * 128], start=True, stop=False),
                         reads=[b_gc], writes=[b_pb[0]])
                    P.op("pe", lambda e, hd=hd: e.matmul(pbank[0][:, 0:128], lhsT=ones1[0:1, :],
                                                         rhs=bup_t[0:1, hd * 128:(hd + 1) * 128], start=False, stop=True),
                         reads=[b_gc], writes=[b_pb[0]])
                    P.op("act", lambda e: e.activation(out=L_t, in_=pbank[0][:, 0:128], func=AF.Exp, scale=-1.0),
                         reads=[b_pb[0]], writes=[b_L])
                    P.op("act", lambda e: e.activation(out=L_t, in_=L_t, func=AF.Ln, bias=1.0), reads=[b_L], writes=[b_L])
                    P.op("pe", lambda e: e.matmul(pbank[1][:, 0:128], lhsT=L_t, rhs=tn_t, start=True, stop=True),
                         reads=[b_L, b_gc], writes=[b_pb[1]])
                    P.op("pe", lambda e: e.matmul(pbank[2][:, 0:128], lhsT=un_t, rhs=L_t, start=True, stop=True),
                         reads=[b_L, b_gc], writes=[b_pb[2]])
                    P.op("act", lambda e: e.activation(out=eq_t, in_=pbank[1][:, 0:128], func=AF.Exp),
                         reads=[b_pb[1]], writes=[b_eq])
                    P.op("act", lambda e: e.activation(out=ek_t, in_=pbank[1][:, 0:128], func=AF.Exp, scale=-1.0),
                         reads=[b_pb[1]], writes=[b_ek])
                    P.op("act", lambda e: e.activation(out=er_t, in_=pbank[2][:, 0:128], func=AF.Exp),
                         reads=[b_pb[2]], writes=[b_er])
                    P.op("dve", lambda e, cs=cs: e.scalar_tensor_tensor(out=qt_b, in0=q_g[:, cs], scalar=SQ, in1=eq_t,
                                                                        op0=ALU.mult, op1=ALU.mult),
                         reads=[b_qg, b_eq], writes=[b_qt])
                    P.op("dve", lambda e, cs=cs: e.tensor_tensor(out=kt_b, in0=k_g[:, cs], in1=ek_t, op=ALU.mult),
                         reads=[b_kg, b_ek], writes=[b_ktb])
                    P.op("dve", lambda e, c=c: e.tensor_tensor(out=kh_b, in0=kt_g[:, c, :], in1=er_t, op=ALU.mult),
                         reads=[b_ktg, b_er], writes=[b_kh])
                    P.op("pe", lambda e: e.matmul(pbank[3][:, 0:128], lhsT=kt_b, rhs=qt_b, start=True, stop=True),
                         reads=[b_ktb, b_qt], writes=[b_pb[3]])
                    P.op("dve", lambda e: e.tensor_tensor(out=at_b, in0=pbank[3][:, 0:128], in1=mtri_t, op=ALU.mult),
                         reads=[b_pb[3], b_gc], writes=[b_at])
                    for ec in range(2):
                        pbo = 4 + ec
                        P.op("pe", lambda e, c=c, ec=ec, pbo=pbo: e.matmul(
                            pbank[pbo][:, 0:128], lhsT=v_g[:, c, ec * 128:(ec + 1) * 128], rhs=at_b,
                            start=True, stop=(c == 0)), reads=[b_vg, b_at], writes=[b_pb[pbo]])
                        if c > 0:
                            P.op("pe", lambda e, ec=ec, pbo=pbo: e.matmul(
                                pbank[pbo][:, 0:128], lhsT=S_b[:, ec * 128:(ec + 1) * 128], rhs=qt_b,
                                start=False, stop=True), reads=[b_S, b_qt], writes=[b_pb[pbo]])
                        P.op("act", lambda e, ec=ec, cs=cs, pbo=pbo: e.activation(out=o_g[:, ec, cs], in_=pbank[pbo][:, 0:128],
                                                                                  func=AF.Copy),
                             reads=[b_pb[pbo]], writes=[b_og])
                    if c < NQB - 1:
                        P.op("pe", lambda e, c=c: e.matmul(pbank[6][:, 0:256], lhsT=kh_b, rhs=v_g[:, c, :],
                                                           start=True, stop=True),
                             reads=[b_kh, b_vg], writes=[b_pb[6]])
                        if c == 0:
                            P.op("dve", lambda e: e.tensor_copy(out=S_f, in_=pbank[6][:, 0:256]),
                                 reads=[b_pb[6]], writes=[b_S])
                        else:
                            P.op("dve", lambda e: e.scalar_tensor_tensor(
                                out=S_f, in0=S_f, scalar=eq_t[:, 127:128], in1=pbank[6][:, 0:256],
                                op0=ALU.mult, op1=ALU.add), reads=[b_S, b_eq, b_pb[6]], writes=[b_S])
                        P.op("dve", lambda e: e.tensor_copy(out=S_b, in_=S_f), reads=[b_S], writes=[b_S])
                for qg in range(S // 512):
                    gs = slice(qg * 512, (qg + 1) * 512)
                    for ec in range(2):
                        P.op("act", lambda e, ec=ec, gs=gs: e.activation(out=sq_b[:, ec, :], in_=o_g[:, ec, gs], func=AF.Square),
                             reads=[b_og], writes=[b_sq])
                    for ec in range(2):
                        P.op("pe", lambda e, ec=ec: e.matmul(pbank[7][:, 0:512], lhsT=ones_b[:], rhs=sq_b[:, ec, :],
                                                             start=(ec == 0), stop=(ec == 1)),
                             reads=[b_sq, b_const], writes=[b_pb[7]])
                    P.op("dve", lambda e: e.tensor_scalar(out=rs_g, in0=pbank[7][:, 0:512], scalar1=1.0 / 256,
                                                          scalar2=RMS_EPS, op0=ALU.mult, op1=ALU.add),
                         reads=[b_pb[7]], writes=[b_rsg])
                    P.op("act", lambda e: e.activation(out=rs_g, in_=rs_g, func=AF.Sqrt), reads=[b_rsg], writes=[b_rsg])
                    P.op("dve", lambda e: e.reciprocal(out=rs_g, in_=rs_g), reads=[b_rsg], writes=[b_rsg])
                    for ec in range(2):
                        P.op("sp", lambda e, hd=hd, ec=ec, gs=gs: e.dma_start(
                            out=gg_t, in_=gfT[(8 + hd * 2 + ec) * 128:(9 + hd * 2 + ec) * 128, gs]),
                            reads=[b_gf], writes=[b_gg], dma=True, lane="glg")
                        P.op("act", lambda e: e.activation(out=gg_t, in_=gg_t, func=AF.Silu), reads=[b_gg], writes=[b_gg])
                        P.op("dve", lambda e, ec=ec: e.scalar_tensor_tensor(
                            out=gg_t, in0=gg_t, scalar=ng_t[:, ec:ec + 1], in1=rs_g, op0=ALU.mult, op1=ALU.mult),
                            reads=[b_gg, b_rsg, b_gc], writes=[b_gg])
                        yo, yob, yol = yogr.next()
                        P.op("dve", lambda e, yo=yo, ec=ec, gs=gs: e.tensor_tensor(out=yo, in0=o_g[:, ec, gs], in1=gg_t, op=ALU.mult),
                             reads=[b_og, b_gg], writes=[yob])
                        P.op("sp", lambda e, yo=yo, hd=hd, ec=ec, gs=gs: e.dma_start(
                            out=yT[(hd * 2 + ec) * 128:(hd * 2 + ec + 1) * 128, gs], in_=yo),
                            reads=[yob], writes=[b_yT], dma=True, lane=yol)

        P.barrier()
        xT_bufs = [b_xT] * KC
        for tt in range(NT):
            t0 = tt * T
            P.op("sp", lambda e, t0=t0: e.dma_start(out=xT[:], in_=yT_v[:, :, t0:t0 + T]),
                 reads=[b_yT], writes=[b_xT], dma=True, lane="yld")
            down_like(xT, xT_bufs, KC, wout_v, zo, tt, 1.0,
                      pcsrc=None if pc is None else pc["wo_b"], pcreads=pcreads)
        P.barrier()

    def final_stage(zi, ln):
        fin_ops = []
        orow = xr_t[0]
        zv = zs[zi].rearrange("(k p) s -> p k s", p=128)
        for tcn in range(S // 128):
            t0 = tcn * 128
            tt = t0 // T
            P.op("sp", lambda e, t0=t0: e.dma_start(out=tr_t, in_=zv[:, :, t0:t0 + 128]),
                 reads=[zbuf[zi][tt]], writes=[b_tr], dma=True, lane="trs")
            for g in range(KC // 4):
                bk = g % 2
                for j in range(4):
                    kc = g * 4 + j
                    P.op("dve", lambda e, kc=kc, t0=t0: e.tensor_tensor(
                        out=tr_t[:, kc, :], in0=tr_t[:, kc, :], in1=mean_bc[:, t0:t0 + 128], op=ALU.subtract),
                        reads=[b_tr, b_stats[tt]], writes=[b_tr])
                    P.op("dve", lambda e, kc=kc, t0=t0: e.tensor_tensor(
                        out=tr_t[:, kc, :], in0=tr_t[:, kc, :], in1=rstd_bc[:, t0:t0 + 128], op=ALU.mult),
                        reads=[b_tr, b_stats[tt]], writes=[b_tr])
                    P.op("act", lambda e, kc=kc: e.activation(
                        out=tr_t[:, kc, :], in_=tr_t[:, kc, :], func=AF.Identity,
                        scale=lng[:, ln, kc:kc + 1], bias=lnb[:, ln, kc:kc + 1]),
                        reads=[b_tr, b_const], writes=[b_tr])
                    P.op("pe", lambda e, kc=kc, j=j, bk=bk: e.transpose(
                        out=pbank[bk][:, j * 128:(j + 1) * 128], in_=tr_t[:, kc, :], identity=ident_f[:]),
                        reads=[b_tr, b_const], writes=[b_pb[bk]])
                if g % 2:
                    P.op("act", lambda e, g=g, bk=bk: e.activation(
                        out=orow[:, g * 512:(g + 1) * 512], in_=pbank[bk][:], func=AF.Copy),
                        reads=[b_pb[bk]], writes=[b_xr])
                else:
                    P.op("dve", lambda e, g=g, bk=bk: e.tensor_copy(
                        out=orow[:, g * 512:(g + 1) * 512], in_=pbank[bk][:]),
                        reads=[b_pb[bk]], writes=[b_xr])
            fo = P.op("sp", lambda e, t0=t0: e.dma_start(out=out[t0:t0 + 128, :], in_=orow),
                      reads=[b_xr], writes=[Buf("out")], dma=True, lane="outst")
            fin_ops.append(fo)
        return fin_ops

    zi = 0
    ln = None
    plan = []
    for l in range(L):
        if "ffn1" in phases:
            plan.append(("ffn", l, 0))
        if "mix" in phases:
            plan.append(("mix", l, 0))
        if "ffn2" in phases:
            plan.append(("ffn", l, 1))
    pcs = [None] * len(plan)
    for i, (kind, l, which) in enumerate(plan):
        if PRECAST and i + 1 < len(plan):
            nk, nl, nw = plan[i + 1]
            pcs[i + 1] = mk_precast_ffn(nl, nw) if nk == "ffn" else mk_precast_mix(nl)
            pacer.load(pcs[i + 1], FFN_STEPS if kind == "ffn" else MIX_STEPS)
        if kind == "ffn":
            ffn(l, which, zi, 1 - zi, ln, pc=pcs[i])
            zi, ln = 1 - zi, l * 3 + (0 if which == 0 else 2)
        else:
            mixer(l, zi, 1 - zi, ln, pc=pcs[i])
            zi, ln = 1 - zi, l * 3 + 1
        pacer.flush()
    P.barrier()
    fin = final_stage(zi, ln)
    P.emit(nc, st, fin[-1:])
    st.close()
    return nc, P


def kernel(**inputs):
    cfg = Cfg()
    nc, _ = build(cfg)
    consts = host_consts()
    L = cfg.L
    x = np.ascontiguousarray(np.asarray(inputs["x"], dtype=np.float32))
    pos = np.ascontiguousarray(np.asarray(inputs["positions"], dtype=np.int32))
    n = x.shape[0]

    def flat(name):
        a = np.asarray(inputs[name], dtype=np.float32)
        return np.ascontiguousarray(a.reshape((a.shape[0] * a.shape[1],) + a.shape[2:]))

    shared = {k: flat(k) for k in ("w_in", "w_out", "ffn1_w_gu", "ffn1_w_down", "ffn2_w_gu", "ffn2_w_down",
                                   "ln_g", "ln_b", "gla_w_gate_up")}
    for k in ("gla_b_gate_up", "gla_norm_g", "swa_sinks"):
        shared[k] = np.ascontiguousarray(np.asarray(inputs[k], dtype=np.float32))
    in_maps = [dict(x=x[i], positions=pos[i:i + 1], **shared, **consts) for i in range(n)]
    res = run_bass_kernel_spmd(nc, in_maps, core_ids=list(range(n)))
    return np.stack([np.asarray(r["out"], dtype=np.float32) for r in res.results], axis=0)
```
